# Optimizing a Trainium2 kernel written in Bass

```python
import jax, jax.numpy as jnp
from jax import lax
import numpy as np

D_MODEL = 4096
BATCH = 4
SEQ = 4096
DEPTH = 1

GLA_HEADS = 8
GLA_DK = 128
GLA_DV = 256
GLA_RANK = 16
GLA_TAU = 16.0
GLA_CHUNK = 64
FOX_HEADS = 16
FOX_DH = 128
FOX_BLOCK = 128
MEM_LEN = 256
XA_HEADS = 4
XA_DH = 256
D_FF = 11008
CONV_W = 3
EPS = 1e-6

GLA_QK = GLA_HEADS * GLA_DK
GLA_VW = GLA_HEADS * GLA_DV
FOX_W = FOX_HEADS * FOX_DH
SPLIT_SIZES = (GLA_QK, GLA_QK, GLA_VW, GLA_VW, GLA_RANK, FOX_W, FOX_W, FOX_W, FOX_HEADS, 2 * D_MODEL)
IN_COLS = sum(SPLIT_SIZES)

kernel_name = "hybrid_gla_fox_xattn_convffn"


def rmsnorm(x, g):
    xf = x.astype(jnp.float32)
    y = xf * lax.rsqrt(jnp.mean(xf * xf, axis=-1, keepdims=True) + EPS)
    return (y * g.astype(jnp.float32)).astype(x.dtype)


def split_cols(t, sizes):
    out, off = [], 0
    for s in sizes:
        out.append(t[..., off:off + s])
        off += s
    return out


def to_heads(t, n):
    b, s, _ = t.shape
    return t.reshape(b, s, n, -1).transpose(0, 2, 1, 3)


def from_heads(t):
    b, n, s, d = t.shape
    return t.transpose(0, 2, 1, 3).reshape(b, s, n * d)


def gla_chunked(q, k, v, lg):
    b, h, s, dk = q.shape
    dv = v.shape[-1]
    c = GLA_CHUNK
    n = s // c

    def chunks(t):
        return t.astype(jnp.float32).reshape(b, h, n, c, t.shape[-1]).transpose(2, 0, 1, 3, 4)

    qc, kc, vc, gc = chunks(q * (dk ** -0.5)), chunks(k), chunks(v), chunks(lg)
    bc = jnp.cumsum(gc, axis=-2)
    causal = jnp.tril(jnp.ones((c, c), dtype=bool))

    def step(state, inp):
        q_, k_, v_, b_ = inp
        o_inter = jnp.einsum("bhik,bhkv->bhiv", q_ * jnp.exp(b_), state)
        diff = b_[:, :, :, None, :] - b_[:, :, None, :, :]
        decay = jnp.exp(jnp.where(causal[:, :, None], diff, -jnp.inf))
        att = jnp.einsum("bhik,bhjk,bhijk->bhij", q_, k_, decay)
        o_intra = jnp.einsum("bhij,bhjv->bhiv", att, v_)
        b_last = b_[:, :, -1:, :]
        k_dec = k_ * jnp.exp(b_last - b_)
        new_state = jnp.exp(b_last)[:, :, 0, :, None] * state + jnp.einsum("bhjk,bhjv->bhkv", k_dec, v_)
        return new_state, o_inter + o_intra

    s0 = jnp.zeros((b, h, dk, dv), jnp.float32)
    _, o = lax.scan(step, s0, (qc, kc, vc, bc))
    return o.transpose(1, 2, 0, 3, 4).reshape(b, h, s, dv).astype(v.dtype)


def fox_attention(q, k, v, logf):
    b, h, s, d = q.shape
    nb = s // FOX_BLOCK
    cum = jnp.cumsum(logf, axis=-1)
    qb = q.reshape(b, h, nb, FOX_BLOCK, d).transpose(2, 0, 1, 3, 4)
    cb = cum.reshape(b, h, nb, FOX_BLOCK).transpose(2, 0, 1, 3)
    kpos = jnp.arange(s)
    scale = d ** -0.5

    def block(inp):
        q_, c_, i = inp
        qpos = i * FOX_BLOCK + jnp.arange(FOX_BLOCK)
        logits = jnp.einsum("bhqd,bhkd->bhqk", q_, k).astype(jnp.float32) * scale
        logits = logits + c_[..., :, None] - cum[:, :, None, :]
        logits = jnp.where(kpos[None, :] <= qpos[:, None], logits, -jnp.inf)
        p = jax.nn.softmax(logits, axis=-1)
        return jnp.einsum("bhqk,bhkd->bhqd", p.astype(v.dtype), v)

    o = lax.map(block, (qb, cb, jnp.arange(nb)))
    return o.transpose(1, 2, 0, 3, 4).reshape(b, h, s, d)


def hybrid_mixer(h, w_in, b_gate, w_gla_gate_up, b_gla_gate, g_gla_norm, b_fox_f,
                 w_gla_branch, w_fox_branch, w_out):
    bsz, s, _ = h.shape
    proj = h @ w_in
    (gq, gk, gv, gr, glow, fq, fk, fv, ff, gates) = split_cols(proj, SPLIT_SIZES)
    lg = jax.nn.log_sigmoid((glow @ w_gla_gate_up + b_gla_gate).astype(jnp.float32)) / GLA_TAU
    o_gla = gla_chunked(to_heads(gq, GLA_HEADS), to_heads(gk, GLA_HEADS),
                        to_heads(gv, GLA_HEADS), to_heads(lg, GLA_HEADS))
    o_gla = from_heads(rmsnorm(o_gla, g_gla_norm)) * jax.nn.silu(gr)
    y_a = o_gla @ w_gla_branch
    logf = jax.nn.log_sigmoid((ff + b_fox_f).astype(jnp.float32)).transpose(0, 2, 1)
    o_fox = fox_attention(to_heads(fq, FOX_HEADS), to_heads(fk, FOX_HEADS),
                          to_heads(fv, FOX_HEADS), logf)
    y_b = from_heads(o_fox) @ w_fox_branch
    g = jax.nn.sigmoid(gates + b_gate).reshape(bsz, s, 2, D_MODEL)
    return (g[..., 0, :] * y_a + g[..., 1, :] * y_b) @ w_out


def cross_attention(h, m, w_q, w_kv, w_o):
    bsz, s, _ = h.shape
    q = (h @ w_q).reshape(bsz, s, XA_HEADS, XA_DH)
    k, v = split_cols(m @ w_kv, (XA_HEADS * XA_DH, XA_HEADS * XA_DH))
    k = k.reshape(bsz, -1, XA_HEADS, XA_DH)
    v = v.reshape(bsz, -1, XA_HEADS, XA_DH)
    logits = jnp.einsum("bshd,bmhd->bhsm", q, k).astype(jnp.float32) * (XA_DH ** -0.5)
    p = jax.nn.softmax(logits, axis=-1)
    o = jnp.einsum("bhsm,bmhd->bshd", p.astype(v.dtype), v).reshape(bsz, s, XA_HEADS * XA_DH)
    return o @ w_o


def conv_ffn(h, w_up, w_conv, b_conv, w_down):
    u = h @ w_up
    ch = u.shape[-1]
    u = lax.conv_general_dilated(u, w_conv[:, None, :], window_strides=(1,),
                                 padding=[(CONV_W - 1, 0)],
                                 dimension_numbers=("NWC", "WIO", "NWC"),
                                 feature_group_count=ch) + b_conv
    gate, up = split_cols(u, (D_FF, D_FF))
    return (jax.nn.gelu(gate, approximate=True) * up) @ w_down


def setup_inputs(seed: int = 0) -> dict:
    key = jax.random.key(seed)
    ks = jax.random.split(key, 32)
    L = DEPTH

    def nrm(k, shape, scale):
        return jax.random.normal(k, shape, jnp.float32) * scale

    def gain(k, shape):
        return 1.0 + 0.02 * jax.random.normal(k, shape, jnp.float32)

    return {
        "x": nrm(ks[0], (BATCH, SEQ, D_MODEL), 1.0),
        "mem": nrm(ks[1], (BATCH, MEM_LEN, D_MODEL), 1.0),
        "g_mix_pre": gain(ks[2], (L, D_MODEL)),
        "w_in": nrm(ks[3], (L, D_MODEL, IN_COLS), D_MODEL ** -0.5),
        "b_gate": nrm(ks[4], (L, 2 * D_MODEL), 0.02),
        "w_gla_gate_up": nrm(ks[5], (L, GLA_RANK, GLA_QK), GLA_RANK ** -0.5),
        "b_gla_gate": nrm(ks[6], (L, GLA_QK), 0.02),
        "g_gla_norm": gain(ks[7], (L, GLA_DV)),
        "b_fox_f": 3.0 + 0.5 * jax.random.normal(ks[8], (L, FOX_HEADS), jnp.float32),
        "w_gla_branch": nrm(ks[9], (L, GLA_VW, D_MODEL), GLA_VW ** -0.5),
        "w_fox_branch": nrm(ks[10], (L, FOX_W, D_MODEL), FOX_W ** -0.5),
        "w_out": nrm(ks[11], (L, D_MODEL, D_MODEL), D_MODEL ** -0.5),
        "g_mix_post": gain(ks[12], (L, D_MODEL)),
        "g_xa_pre": gain(ks[13], (L, D_MODEL)),
        "g_mem": gain(ks[14], (L, D_MODEL)),
        "w_xa_q": nrm(ks[15], (L, D_MODEL, XA_HEADS * XA_DH), D_MODEL ** -0.5),
        "w_xa_kv": nrm(ks[16], (L, D_MODEL, 2 * XA_HEADS * XA_DH), D_MODEL ** -0.5),
        "w_xa_o": nrm(ks[17], (L, XA_HEADS * XA_DH, D_MODEL), (XA_HEADS * XA_DH) ** -0.5),
        "g_xa_post": gain(ks[18], (L, D_MODEL)),
        "g_ffn_pre": gain(ks[19], (L, D_MODEL)),
        "w_ffn_up": nrm(ks[20], (L, D_MODEL, 2 * D_FF), D_MODEL ** -0.5),
        "w_conv": nrm(ks[21], (L, CONV_W, 2 * D_FF), CONV_W ** -0.5),
        "b_conv": nrm(ks[22], (L, 2 * D_FF), 0.02),
        "w_ffn_down": nrm(ks[23], (L, D_FF, D_MODEL), D_FF ** -0.5),
        "g_ffn_post": gain(ks[24], (L, D_MODEL)),
    }


def reference(x, mem, g_mix_pre, w_in, b_gate, w_gla_gate_up, b_gla_gate, g_gla_norm, b_fox_f,
              w_gla_branch, w_fox_branch, w_out, g_mix_post, g_xa_pre, g_mem, w_xa_q, w_xa_kv,
              w_xa_o, g_xa_post, g_ffn_pre, w_ffn_up, w_conv, b_conv, w_ffn_down, g_ffn_post):
    for l in range(DEPTH):
        h = rmsnorm(x, g_mix_pre[l])
        y = hybrid_mixer(h, w_in[l], b_gate[l], w_gla_gate_up[l], b_gla_gate[l], g_gla_norm[l],
                         b_fox_f[l], w_gla_branch[l], w_fox_branch[l], w_out[l])
        x = x + rmsnorm(y, g_mix_post[l])
        h = rmsnorm(x, g_xa_pre[l])
        m = rmsnorm(mem, g_mem[l])
        y = cross_attention(h, m, w_xa_q[l], w_xa_kv[l], w_xa_o[l])
        x = x + rmsnorm(y, g_xa_post[l])
        h = rmsnorm(x, g_ffn_pre[l])
        y = conv_ffn(h, w_ffn_up[l], w_conv[l], b_conv[l], w_ffn_down[l])
        x = x + rmsnorm(y, g_ffn_post[l])
    return x
```

```python
import bisect
from contextlib import ExitStack
import numpy as np
import concourse.bass as bass
import concourse.mybir as mybir
from concourse.bass_utils import run_bass_kernel_spmd

F32 = mybir.dt.float32
BF16 = mybir.dt.bfloat16
AF = mybir.ActivationFunctionType
ALU = mybir.AluOpType
EPS = 1e-6
NEG_BIG = -30000.0


def make_cfg(D=4096, S=2048, GH=8, FH=16, XH=4, MEM=256, DFF=11008):
    c = dict(D=D, S=S, GH=GH, FH=FH, XH=XH, MEM=MEM, DFF=DFF)
    c["KC"] = D // 128
    c["NB"] = S // 128
    c["GQK"] = GH * 128
    c["GVW"] = GH * 256
    c["FW"] = FH * 128
    c["XW"] = XH * 256
    c["FC"] = DFF // 128
    c["IN_COLS"] = 2 * c["GQK"] + 2 * c["GVW"] + 16 + 3 * c["FW"] + FH + 2 * D
    c["T"] = min(1024, S)
    return c


class Sem:
    def __init__(self, h):
        self.h = h
        self.count = 0


class EngRec:
    def __init__(self, name, sem):
        self.name = name
        self.sem = sem
        self.items = []
        self.idx = 0
        self.sig_idx = []
        self.sig_val = []
        self.seen = {}


class Slot:
    __slots__ = ("name", "w", "r", "sem")

    def __init__(self, name, sem=None):
        self.name = name
        self.w = {}
        self.r = {}
        self.sem = sem


class Prog:
    def __init__(self, nc, sems):
        self.nc = nc
        self.pool = [Sem(h) for h in sems]
        self.E = {n: EngRec(n, self.pool.pop()) for n in ("pe", "act", "dve", "pool", "sp")}
        self.phase_sems = []
        self.used_dma_sems = []

    def new_sem(self):
        s = self.pool.pop()
        self.phase_sems.append(s)
        return s

    def slot(self, name, dma=False):
        return Slot(name, self.new_sem() if dma else None)

    def _resolve(self, tok):
        if tok[0] == "s":
            return tok[1], tok[2]
        e, idx = tok[1], tok[2]
        k = bisect.bisect_left(e.sig_idx, idx)
        assert k < len(e.sig_idx), f"no signal after idx {idx} on {e.name}"
        return e.sem, e.sig_val[k]

    def _wait(self, X, tok):
        if tok is None:
            return
        if tok[0] == "e" and tok[1] is X and X.name == "pe":
            return
        sem, val = self._resolve(tok)
        if X.seen.get(sem, 0) >= val:
            return
        X.seen[sem] = val
        X.items.append(("w", sem, val))

    def _deps(self, X, reads, writes):
        for s in reads:
            for t in s.w.values():
                self._wait(X, t)
        for s in writes:
            for t in s.w.values():
                self._wait(X, t)
            for t in s.r.values():
                self._wait(X, t)

    def op(self, e, fn, reads=(), writes=(), signal=True):
        X = self.E[e]
        self._deps(X, reads, writes)
        tok = ("e", X, X.idx)
        if signal:
            X.sem.count += 1
            X.sig_idx.append(X.idx)
            X.sig_val.append(X.sem.count)
            X.items.append(("o", fn, X.sem, 1))
        else:
            X.items.append(("o", fn, None, 0))
        X.idx += 1
        for s in reads:
            s.r[e] = tok
        for s in writes:
            s.w[e] = tok
            s.r = {}
        return tok

    def dma(self, q, out, in_, reads=(), writes=(), sem=None):
        X = self.E[q]
        if sem is None:
            sem = (writes[0] if writes else reads[0]).sem
        assert sem is not None
        skip = ("d", id(sem))
        for s in reads:
            for t in s.w.values():
                self._wait(X, t)
        for s in writes:
            for k, t in s.w.items():
                if k != skip:
                    self._wait(X, t)
            for t in s.r.values():
                self._wait(X, t)
        sem.count += 16
        tok = ("s", sem, sem.count)
        X.items.append(("o", lambda eng, o=out, i=in_: eng.dma_start(out=o, in_=i), sem, 16))
        X.idx += 1
        for s in reads:
            s.r[("d", id(sem))] = tok
        for s in writes:
            s.w[("d", id(sem))] = tok
            s.r = {}
        return tok

    def barrier(self):
        toks = []
        for n in ("pe", "act", "dve", "pool"):
            e = self.E[n]
            if e.sig_idx:
                toks.append(("s", e.sem, e.sig_val[-1]))
        for s in self.phase_sems:
            if s.count:
                toks.append(("s", s, s.count))
        for n in ("pe", "act", "dve", "pool", "sp"):
            X = self.E[n]
            for t in toks:
                sem, val = t[1], t[2]
                if X.seen.get(sem, 0) >= val:
                    continue
                X.seen[sem] = val
                X.items.append(("w", sem, val))
        self.pool.extend(self.phase_sems)
        self.phase_sems = []

    def emit(self, block):
        def run(X):
            def f(eng):
                for it in X.items:
                    if it[0] == "w":
                        eng.wait_ge(it[1].h, it[2])
                    else:
                        ins = it[1](eng)
                        if it[2] is not None:
                            ins.then_inc(it[2].h, it[3])
            return f
        block.tensor(run(self.E["pe"]))
        block.scalar(run(self.E["act"]))
        block.vector(run(self.E["dve"]))
        block.gpsimd(run(self.E["pool"]))
        block.sync(run(self.E["sp"]))


class Arena:
    def __init__(self, ap, words):
        self.ap = ap
        self.words = words
        self.off = 0

    def reset(self):
        self.off = 0

    def alloc(self, shape, dt, parts=128):
        n = int(np.prod(shape))
        sz = 2 if dt == BF16 else 4
        w = (n * sz + 3) // 4
        w = (w + 7) // 8 * 8
        assert self.off + w <= self.words, f"SBUF arena overflow {self.off + w} > {self.words}"
        a = self.ap[0:parts, self.off:self.off + w]
        self.off += w
        if dt == BF16:
            a = a.bitcast(BF16)
        a = a[:, 0:n]
        if len(shape) == 2:
            a = a.rearrange("p (a b) -> p a b", a=shape[0])
        elif len(shape) == 3:
            a = a.rearrange("p (a b c) -> p a b c", a=shape[0], b=shape[1])
        return a


class Banks:
    def __init__(self, P, ps):
        self.ps = ps
        self.slots = [Slot(f"bank{i}") for i in range(8)]
        self.ptr = 0

    def get(self, n=1):
        if n > 1:
            self.ptr = (self.ptr + n - 1) // n * n
        if self.ptr + n > 8:
            self.ptr = 0
        b = self.ptr
        self.ptr = (self.ptr + n) % 8
        return b, self.slots[b:b + n]

    def f32(self, b, n=1):
        return self.ps[:, b * 512:(b + n) * 512]

    def bf(self, b, n=1):
        return self.ps[:, b * 512:(b + n) * 512].bitcast(BF16)


def build_program(cfg, debug_outs=(), upto=99):
    D, S, GH, FH, XH, MEM, DFF = (cfg[k] for k in ("D", "S", "GH", "FH", "XH", "MEM", "DFF"))
    KC, NB, GQK, GVW, FW, XW, FC, IN_COLS, T = (cfg[k] for k in ("KC", "NB", "GQK", "GVW", "FW", "XW", "FC", "IN_COLS", "T"))
    NTT = S // T
    TG = T // 512
    BRC = GVW // 128 + FH

    nc = bass.Bass("TRN2", target_bir_lowering=False)

    def din(name, shape, dt=F32):
        return nc.dram_tensor(name, list(shape), dt, kind="ExternalInput").ap()

    def dscr(name, shape, dt):
        kind = "ExternalOutput" if name in debug_outs else "Internal"
        return nc.dram_tensor(name, list(shape), dt, kind=kind).ap()

    x = din("x", [S, D])
    mem = din("mem", [MEM, D])
    w_in = din("w_in", [D, IN_COLS])
    w_gb = din("w_gla_branch", [GVW, D])
    w_fb = din("w_fox_branch", [FW, D])
    w_out = din("w_out", [D, D])
    w_xq = din("w_xa_q", [D, XW])
    w_xkv = din("w_xa_kv", [D, 2 * XW])
    w_xo = din("w_xa_o", [XW, D])
    w_up = din("w_ffn_up", [D, 2 * DFF])
    w_dn = din("w_ffn_down", [DFF, D])
    gcols = din("gcols", [128, 4 * KC])
    gpost = din("gpost", [3, D])
    bgate = din("bgate", [128, 2 * KC])
    wgu = din("wgu", [17, GQK])
    ggla = din("ggla", [1, GVW])
    bfox = din("bfox", [1, FH])
    wconv = din("wconv", [128, 3 * 2 * FC])
    bconv = din("bconv", [128, 2 * FC])
    consts = din("consts", [128, 5 * 128])
    flags = din("flags", [128, 2])
    out = nc.dram_tensor("out", [S, D], F32, kind="ExternalOutput").ap()

    hT = dscr("hT", [KC, 128, S], BF16)
    tm1 = dscr("tm1", [S, 2 * GQK + 2 * GVW], BF16)
    glowT = dscr("glowT", [16, S], F32)
    fqT = dscr("fqT", [FH * 128, S], BF16)
    ffd = dscr("ffd", [S, FH], F32)
    sgT = dscr("sgT", [2 * KC, 128, S], BF16)
    brT = dscr("brT", [BRC, 128, S], BF16)
    mT = dscr("mT", [KC, 128, S], BF16)
    yb = dscr("yb", [S, D], F32)
    memT = dscr("memT", [KC, 128, MEM], BF16)
    kxT = dscr("kxT", [XW // 128, 128, MEM], BF16)
    vx = dscr("vx", [MEM, XW], BF16)
    qxT = dscr("qxT", [XW // 128, 128, S], BF16)
    oxT = dscr("oxT", [XW // 128, 128, S], BF16)
    aT = dscr("aT", [FC, 128, S], BF16)
    gin_fk = nc.dram_tensor("gin_fk", [FH * 128, S], BF16).ap()
    CCB = 2 * 1024 * 1024
    fk_rows = max(128, min(FH * 128, CCB // (S * 2) // 128 * 128))
    gout_fk = [nc.dram_tensor(f"gout_fk{i}", [2 * fk_rows, S], BF16).ap() for i in range(FH * 128 // fk_rows)]
    gin_fv = nc.dram_tensor("gin_fv", [S, FW], BF16).ap()
    fv_rows = max(128, min(S, CCB // (FW * 2) // 128 * 128))
    gout_fv = [nc.dram_tensor(f"gout_fv{i}", [2 * fv_rows, FW], BF16).ap() for i in range(S // fv_rows)]
    gin_rs = nc.dram_tensor("gin_rs", [128, NB * FH], F32).ap()
    gout_rs = nc.dram_tensor("gout_rs", [256, NB * FH], F32).ap()
    gin_st = nc.dram_tensor("gin_st", [128, GH * 256], F32).ap()
    gout_st = nc.dram_tensor("gout_st", [256, GH * 256], F32).ap()
    gin_hl = nc.dram_tensor("gin_hl", [128, KC * 2], BF16).ap()
    gout_hl = nc.dram_tensor("gout_hl", [256, KC * 2], BF16).ap()
    dbg = {}
    if "dbg_st" in debug_outs:
        dbg["st"] = nc.dram_tensor("dbg_st", [128, GH * 256], F32, kind="ExternalOutput").ap()

    ARENA_WORDS = 51200
    es = ExitStack()
    arena_t = es.enter_context(nc.sbuf_tensor("arena", [128, ARENA_WORDS], F32))
    psum_t = es.enter_context(nc.psum_tensor("psum", [128, 8 * 512], F32))
    sems = [es.enter_context(nc.semaphore(f"s{i}")) for i in range(90)]
    P = Prog(nc, sems)
    A = Arena(arena_t[:], ARENA_WORDS)
    BK = Banks(P, psum_t[:])
    PAIRS = [[0, 1], [2, 3], [4, 5], [6, 7]]

    def mm(out, lhsT, rhs, start, stop, reads, writes, signal, skip=False):
        if skip:
            P.op("pe", lambda e, o=out, l=lhsT, r=rhs, s=start, t=stop: e.matmul(o, l, r, start=s, stop=t, skip_group_check=True),
                 reads=reads, writes=writes, signal=signal)
        else:
            P.op("pe", lambda e, o=out, l=lhsT, r=rhs, s=start, t=stop: e.matmul(o, l, r, start=s, stop=t),
                 reads=reads, writes=writes, signal=signal)

    def tr(out, in_, ident, reads, writes, signal):
        P.op("pe", lambda e, o=out, i=in_, d=ident: e.transpose(o, i, d), reads=reads, writes=writes, signal=signal)

    def act(out, in_, func, reads, writes, bias=None, scale=None, accum=None):
        kw = {}
        if bias is not None:
            kw["bias"] = bias
        if scale is not None:
            kw["scale"] = scale
        if accum is not None:
            kw["accum_out"] = accum
        P.op("act", lambda e, o=out, i=in_, f=func, k=kw: e.activation(out=o, in_=i, func=f, **k), reads=reads, writes=writes)

    def tt(eng, out, in0, in1, op, reads, writes):
        P.op(eng, lambda e, o=out, a=in0, b=in1, p=op: e.tensor_tensor(o, a, b, p), reads=reads, writes=writes)

    def ts(eng, out, in0, s1, s2, op0, op1, reads, writes):
        if op1 is None:
            P.op(eng, lambda e, o=out, a=in0, x=s1, p=op0: e.tensor_scalar(o, a, x, None, p), reads=reads, writes=writes)
        else:
            P.op(eng, lambda e, o=out, a=in0, x=s1, y=s2, p=op0, q=op1: e.tensor_scalar(o, a, x, y, p, q), reads=reads, writes=writes)

    def stt(eng, out, in0, sc, in1, op0, op1, reads, writes):
        P.op(eng, lambda e, o=out, a=in0, s=sc, b=in1, p=op0, q=op1: e.scalar_tensor_tensor(o, a, s, b, p, q), reads=reads, writes=writes)

    def cp(eng, out, in_, reads, writes):
        if eng == "act":
            P.op("act", lambda e, o=out, i=in_: e.copy(o, i), reads=reads, writes=writes)
        else:
            P.op(eng, lambda e, o=out, i=in_: e.tensor_copy(o, i), reads=reads, writes=writes)

    def memset(eng, ap, val, writes):
        P.op(eng, lambda e, a=ap, v=val: e.memset(a, v), writes=writes)

    def begin_phase():
        A.reset()

    def end_phase():
        P.barrier()

    def load_consts(which):
        r = {}
        sl = P.slot("consts", dma=True)
        if "ident" in which:
            r["ident"] = A.alloc([128], BF16)
            P.dma("pool", r["ident"], consts[:, 0:128], writes=[sl])
        if "cmask_bf" in which:
            r["cmask_bf"] = A.alloc([128], BF16)
            P.dma("pool", r["cmask_bf"], consts[:, 384:512], writes=[sl])
        if "ones_bf" in which:
            r["ones_bf"] = A.alloc([128], BF16)
            P.dma("pool", r["ones_bf"], consts[:, 512:640], writes=[sl])
        if "f32" in which:
            r["f32"] = A.alloc([5, 128], F32)
            P.dma("sp", r["f32"], consts.rearrange("p (a b) -> p a b", a=5), writes=[sl])
        if "flags" in which:
            r["flags"] = A.alloc([2], F32)
            P.dma("sp", r["flags"], flags, writes=[sl])
        r["slot"] = sl
        return r

    def nr_phase(x_src, n_tok, y_src, gpost_idx, gpre_idx, x_dst, hT_dst, halo=False):
        begin_phase()
        nb = n_tok // 128
        C = load_consts(["ident"])
        gb = gbs = None
        if y_src is not None:
            gb = A.alloc([D], F32)
            gbs = P.slot("gb", dma=True)
            P.dma("sp", gb, gpost[gpost_idx:gpost_idx + 1, :].partition_broadcast(128), writes=[gbs])
        gc = gcs = None
        if hT_dst is not None:
            gc = A.alloc([4 * KC], F32)
            gcs = P.slot("gc", dma=True)
            P.dma("sp", gc, gcols, writes=[gcs])
        xb = [(A.alloc([D], F32), P.slot("x", dma=True)) for _ in range(2)]
        ybuf = [(A.alloc([D], F32), P.slot("y", dma=True)) for _ in range(2)] if y_src is not None else None
        junk = A.alloc([D], BF16)
        junks = P.slot("junk")
        hn = A.alloc([D], BF16)
        hns = P.slot("hn")
        st = A.alloc([8], F32)
        sts = P.slot("st")
        tgw = min(512, n_tok)
        hst = [(A.alloc([KC, tgw], BF16), P.slot("hst", dma=True)) for _ in range(1)] if hT_dst is not None else None
        hl = hls = None
        if halo:
            hl = A.alloc([KC, 2], BF16)
            hls = P.slot("hl", dma=True)
        for tb in range(nb):
            xt, xs = xb[tb % 2]
            r0 = tb * 128
            P.dma("sp", xt, x_src[r0:r0 + 128, :], writes=[xs])
            if y_src is not None:
                yt, ys = ybuf[tb % 2]
                P.dma("sp", yt, y_src[r0:r0 + 128, :], writes=[ys])
                act(junk, yt, AF.Square, [ys], [junks, sts], accum=st[:, 0:1])
                ts("dve", st[:, 1:2], st[:, 0:1], 1.0 / D, EPS, ALU.mult, ALU.add, [sts], [sts])
                act(st[:, 1:2], st[:, 1:2], AF.Sqrt, [sts], [sts])
                P.op("dve", lambda e, o=st[:, 1:2], i_=st[:, 1:2]: e.reciprocal(o, i_), reads=[sts], writes=[sts])
                stt("dve", yt, yt, st[:, 1:2], gb, ALU.mult, ALU.mult, [ys, sts, gbs], [ys])
                tt("dve", xt, yt, xt, ALU.add, [ys, xs], [xs])
            if x_dst is not None:
                P.dma("sp", x_dst[r0:r0 + 128, :], xt, reads=[xs])
            if hT_dst is not None:
                act(junk, xt, AF.Square, [xs], [junks, sts], accum=st[:, 2:3])
                ts("dve", st[:, 3:4], st[:, 2:3], 1.0 / D, EPS, ALU.mult, ALU.add, [sts], [sts])
                act(st[:, 3:4], st[:, 3:4], AF.Sqrt, [sts], [sts])
                P.op("dve", lambda e, o=st[:, 3:4], i_=st[:, 3:4]: e.reciprocal(o, i_), reads=[sts], writes=[sts])
                act(hn, xt, AF.Copy, [xs, sts], [hns], scale=st[:, 3:4])
                hs_t, hs_s = hst[0]
                tcol = (tb * 128) % tgw
                for k0 in range(0, KC, 8):
                    kn = min(8, KC - k0)
                    b, bs = BK.get(1)
                    pst = BK.bf(b)
                    for k in range(kn):
                        tr(pst[:, k * 128:(k + 1) * 128], hn[:, (k0 + k) * 128:(k0 + k + 1) * 128], C["ident"],
                           [hns, C["slot"]], bs, k == kn - 1)
                    for k in range(kn):
                        kc = k0 + k
                        gsc = gc[:, gpre_idx * KC + kc:gpre_idx * KC + kc + 1]
                        evac(hs_t[:, kc, tcol:tcol + 128], pst[:, k * 128:(k + 1) * 128], bs + [gcs], [hs_s], scale=gsc)
                if halo and tb == nb - 1:
                    cp("dve", hl, hs_t[:, :, tgw - 2:tgw], [hs_s], [hls])
                    P.dma("sp", gin_hl, hl.rearrange("p a b -> p (a b)"), reads=[hls])
                if tcol + 128 == tgw:
                    t0 = tb * 128 + 128 - tgw
                    P.dma("sp", hT_dst.rearrange("k p t -> p k t")[:, :, t0:t0 + tgw], hs_t, reads=[hs_s])
        end_phase()

    def gemm_phase(a_src, akc, n_tok, Tt, wblocks, extra_setup=None, wbuf_kc=None, nwbuf=3):
        begin_phase()
        ctx = extra_setup() if extra_setup is not None else {}
        ntt = n_tok // Tt
        at = A.alloc([akc, Tt], BF16)
        ats = P.slot("A", dma=True)
        wkc = wbuf_kc or akc
        wb = [(A.alloc([wkc, 512], BF16), P.slot(f"w{i}", dma=True)) for i in range(nwbuf)]
        ctx["at"], ctx["ats"] = at, ats
        items = [(tti, blk) for tti in range(ntt) for blk in wblocks]

        def issue(k):
            wt, ws = wb[k % nwbuf]
            for (Wap, k0, nk, c0, ncols, wk0, wc0) in items[k][1]["parts"]:
                Wv = Wap.rearrange("(k p) n -> p k n", p=128)
                for kk in range(0, nk, 8):
                    kn = min(8, nk - kk)
                    P.dma("pool", wt[:, wk0 + kk:wk0 + kk + kn, wc0:wc0 + ncols],
                          Wv[:, k0 + kk:k0 + kk + kn, c0:c0 + ncols], writes=[ws])

        for k in range(min(nwbuf - 1, len(items))):
            issue(k)
        cur_tt = -1
        for k, (tti, blk) in enumerate(items):
            t0 = tti * Tt
            if tti != cur_tt:
                cur_tt = tti
                for k0 in range(0, akc, 8):
                    kn = min(8, akc - k0)
                    P.dma("sp", at[:, k0:k0 + kn, :], a_src.rearrange("k p t -> p k t")[:, k0:k0 + kn, t0:t0 + Tt], writes=[ats])
            if k + nwbuf - 1 < len(items):
                issue(k + nwbuf - 1)
            wt, ws = wb[k % nwbuf]
            for item in blk["tiles"](tti, t0, wt, ws, ctx):
                mmlist, epi = item[0], item[1]
                extra = item[2] if len(item) > 2 else []
                nbk = 1 + max(m[0] for m in mmlist)
                b, bs = BK.get(nbk)
                started = set()
                last_idx = {}
                for i, m in enumerate(mmlist):
                    last_idx[m[0]] = i
                for i, (bi, o_fn, lhsT, rhs) in enumerate(mmlist):
                    mm(o_fn(b + bi), lhsT, rhs, bi not in started, last_idx[bi] == i,
                       [ats, ws] + extra, [bs[bi]], last_idx[bi] == i)
                    started.add(bi)
                epi(b, bs)
        end_phase()

    class Stage:
        def __init__(self, n, shape, dt, name):
            self.bufs = [(A.alloc(shape, dt), P.slot(name, dma=True)) for _ in range(n)]
            self.i = 0

        def next(self):
            r = self.bufs[self.i % len(self.bufs)]
            self.i += 1
            return r

    ev_ctr = [0]

    def evac(out, in_, reads, writes, scale=None):
        ev_ctr[0] += 1
        if ev_ctr[0] % 2 == 0:
            if scale is None:
                cp("act", out, in_, reads, writes)
            else:
                P.op("act", lambda e, o=out, i=in_, s=scale: e.mul(o, i, s), reads=reads, writes=writes)
        else:
            if scale is None:
                cp("dve", out, in_, reads, writes)
            else:
                ts("dve", out, in_, scale, None, ALU.mult, None, reads, writes)

    def tm_tiles(Tt, nk, ncols, wk0, epi_fn, akc_off=0):
        def f(tti, t0, wt, ws, ctx):
            at = ctx["at"]
            res = []
            for tb in range(Tt // 128):
                ml = [(0, (lambda b, n=ncols: BK.f32(b)[:, 0:n]), at[:, akc_off + k, tb * 128:(tb + 1) * 128],
                       wt[:, wk0 + k, 0:ncols]) for k in range(nk)]
                res.append((ml, (lambda b, bs, r0=t0 + tb * 128: epi_fn(b, bs, r0))))
            return res
        return f

    def fm_tiles(Tt, nk, ncols, wk0, epi_fn, akc_off=0):
        def f(tti, t0, wt, ws, ctx):
            at = ctx["at"]
            res = []
            tgw = min(512, Tt)
            for sub in range((ncols + 127) // 128):
                cw = min(128, ncols - sub * 128)
                for tg in range(Tt // tgw):
                    ml = [(0, (lambda b, c=cw, w=tgw: BK.f32(b)[0:c, 0:w]), wt[:, wk0 + k, sub * 128:sub * 128 + cw],
                           at[:, akc_off + k, tg * tgw:(tg + 1) * tgw]) for k in range(nk)]
                    res.append((ml, (lambda b, bs, s=sub, tt0=t0 + tg * tgw, c=cw, w=tgw: epi_fn(b, bs, s, tt0, c, w))))
            return res
        return f

    if upto > 0:
        nr_phase(x, S, None, 0, 0, None, hT)

    def g1_blocks():
        blocks = []
        stage = {}

        def setup():
            stage["tm"] = Stage(3, [512], BF16, "stm")
            stage["fm"] = Stage(3, [512], BF16, "sfm")
            stage["f32"] = Stage(2, [512], F32, "sf32")
            bg = A.alloc([2 * KC], F32)
            bgs = P.slot("bg", dma=True)
            P.dma("sp", bg, bgate, writes=[bgs])
            stage["bg"], stage["bgs"] = bg, bgs
            return {}

        def tm_store(dst, c0, ncols, dt=BF16):
            def epi(b, bs, r0):
                st_, ss = stage["tm" if dt == BF16 else "f32"].next()
                evac(st_[:, 0:ncols], BK.f32(b)[:, 0:ncols], bs, [ss])
                P.dma("sp", dst[r0:r0 + 128, c0:c0 + ncols], st_[:, 0:ncols], reads=[ss])
            return epi

        def fm_store(dstT_rows, row0, scale=None, dt=BF16):
            def epi(b, bs, sub, tt0, cw, w):
                st_, ss = stage["fm" if dt == BF16 else "f32"].next()
                evac(st_[0:cw, 0:w], BK.f32(b)[0:cw, 0:w], bs, [ss], scale=scale)
                P.dma("sp", dstT_rows[row0 + sub * 128:row0 + sub * 128 + cw, tt0:tt0 + w], st_[0:cw, 0:w], reads=[ss])
            return epi

        def gate_store(cblk):
            def epi(b, bs, sub, tt0, cw, w):
                st_, ss = stage["fm"].next()
                ch = cblk * 4 + sub
                act(st_[:, 0:w], BK.f32(b)[:, 0:w], AF.Sigmoid, bs + [stage["bgs"]], [ss], bias=stage["bg"][:, ch:ch + 1])
                P.dma("sp", sgT[ch, :, tt0:tt0 + w], st_[:, 0:w], reads=[ss])
            return epi

        col = 0
        n_tm1 = 2 * GQK + 2 * GVW
        for c0 in range(0, n_tm1, 512):
            blocks.append(dict(parts=[(w_in, 0, KC, c0, 512, 0, 0)], tiles=tm_tiles(T, KC, 512, 0, tm_store(tm1, c0, 512))))
        col = n_tm1
        blocks.append(dict(parts=[(w_in, 0, KC, col, 16, 0, 0)], tiles=fm_tiles(T, KC, 16, 0, fm_store(glowT, 0, dt=F32))))
        col += 16
        for c0 in range(0, FW, 512):
            blocks.append(dict(parts=[(w_in, 0, KC, col + c0, 512, 0, 0)],
                               tiles=fm_tiles(T, KC, 512, 0, fm_store(fqT, c0, scale=128 ** -0.5))))
        col += FW
        for c0 in range(0, FW, 512):
            blocks.append(dict(parts=[(w_in, 0, KC, col + c0, 512, 0, 0)], tiles=fm_tiles(T, KC, 512, 0, fm_store(gin_fk, c0))))
        col += FW
        for c0 in range(0, FW, 512):
            blocks.append(dict(parts=[(w_in, 0, KC, col + c0, 512, 0, 0)], tiles=tm_tiles(T, KC, 512, 0, tm_store(gin_fv, c0, 512))))
        col += FW
        blocks.append(dict(parts=[(w_in, 0, KC, col, FH, 0, 0)], tiles=tm_tiles(T, KC, FH, 0, tm_store(ffd, 0, FH, dt=F32))))
        col += FH
        for cb in range(2 * D // 512):
            blocks.append(dict(parts=[(w_in, 0, KC, col + cb * 512, 512, 0, 0)], tiles=fm_tiles(T, KC, 512, 0, gate_store(cb))))
        return blocks, setup

    blocks, setup = g1_blocks()
    if upto > 1:
        gemm_phase(hT, KC, S, T, blocks, extra_setup=setup)

    def fox_prep():
        begin_phase()
        C = load_consts(["f32"])
        cf = C["f32"]
        l = A.alloc([NB, FH], F32)
        ls = P.slot("l", dma=True)
        P.dma("sp", l, ffd.rearrange("(n p) h -> p n h", p=128), writes=[ls])
        bf_ = A.alloc([FH], F32)
        bfs = P.slot("bf", dma=True)
        P.dma("sp", bf_, bfox.partition_broadcast(128), writes=[bfs])
        for n in range(NB):
            tt("dve", l[:, n, :], l[:, n, :], bf_, ALU.add, [ls, bfs], [ls])
        l2 = l.rearrange("p n h -> p (n h)")
        act(l2, l2, AF.Exp, [ls], [ls], scale=-1.0)
        act(l2, l2, AF.Ln, [ls], [ls], bias=1.0)
        b, bs = BK.get(1)
        NF = NB * FH
        mm(BK.f32(b)[:, 0:NF], cf[:, 3, :], l2, True, True, [C["slot"], ls], bs, True)
        b2, bs2 = BK.get(1)
        mm(BK.f32(b2)[:, 0:NF], cf[:, 4, :], l2, True, True, [C["slot"], ls], bs2, True)
        tot = A.alloc([NB, FH], F32)
        tots = P.slot("tot")
        cp("dve", tot.rearrange("p n h -> p (n h)"), BK.f32(b2)[:, 0:NF], bs2, [tots])
        off = A.alloc([NB + 1, FH], F32)
        offs = P.slot("off")
        memset("dve", off[:, 0, :], 0.0, [offs])
        for n in range(NB):
            tt("dve", off[:, n + 1, :], off[:, n, :], tot[:, n, :], ALU.add, [offs, tots], [offs])
        lc = A.alloc([NB, FH], F32)
        lcs = P.slot("lc", dma=True)
        tt("dve", lc.rearrange("p n h -> p (n h)"), BK.f32(b)[:, 0:NF], off[:, 0:NB, :].rearrange("p n h -> p (n h)"),
           ALU.add, bs + [offs], [lcs])
        rs = A.alloc([NB, FH], F32)
        rss = P.slot("rs", dma=True)
        for n in range(NB):
            tt("dve", rs[:, n, :], lc[:, n, :], off[:, NB, :], ALU.subtract, [lcs, offs], [rss])
        P.dma("sp", gin_rs, rs.rearrange("p n h -> p (n h)"), reads=[rss])
        end_phase()

    if upto > 2:
        fox_prep()

    def gla_scan(prescan):
        begin_phase()
        C = load_consts(["ident", "cmask_bf", "f32", "flags"])
        cf = C["f32"]
        cs = C["slot"]
        W1 = 2 * GQK + 2 * GVW
        glw = A.alloc([S], F32, parts=32)
        gls = P.slot("glw", dma=True)
        memset("dve", glw, 1.0, [gls])
        P.dma("sp", glw[0:16, :], glowT, writes=[gls])
        wg = A.alloc([GQK], F32, parts=32)
        wgs = P.slot("wg", dma=True)
        P.dma("sp", wg[0:17, :], wgu, writes=[wgs])
        Sst = A.alloc([GH, 256], F32)
        Sbf = A.alloc([GH, 256], BF16)
        Ss = [P.slot(f"S{h}") for h in range(GH)]
        Sbs = [P.slot(f"Sb{h}") for h in range(GH)]
        Sld = P.slot("Sld", dma=True)
        if prescan:
            memset("dve", Sst.rearrange("p a b -> p (a b)"), 0.0, Ss)
        else:
            P.dma("sp", Sst.rearrange("p a b -> p (a b)"), gout_st[0:128, :], writes=[Sld])
            for h in range(GH):
                ts("dve", Sst[:, h, :], Sst[:, h, :], C["flags"][:, 0:1], None, ALU.mult, None, [Sld, cs], [Ss[h]])
                cp("act", Sbf[:, h, :], Sst[:, h, :], [Ss[h]], [Sbs[h]])
            gg = A.alloc([GVW], F32)
            ggs = P.slot("gg", dma=True)
            P.dma("sp", gg, ggla.partition_broadcast(128), writes=[ggs])
        tmb = [(A.alloc([W1], BF16), P.slot("tm", dma=True)) for _ in range(2)]
        e_ = A.alloc([GQK], F32)
        es_ = P.slot("e")
        l_ = A.alloc([GQK], F32)
        ls_ = P.slot("l")
        ed = A.alloc([GQK], F32)
        eds = P.slot("ed")
        kd = A.alloc([GQK], BF16)
        kds = P.slot("kd")
        dec = A.alloc([GH], F32)
        decs = P.slot("dec")
        if not prescan:
            eb = A.alloc([GQK], F32)
            ebs = P.slot("eb")
            enb = A.alloc([GQK], F32)
            enbs = P.slot("enb")
            qt_ = A.alloc([GQK], BF16)
            qts = P.slot("qt")
            kt_ = A.alloc([GQK], BF16)
            kts = P.slot("kt")
            qT = A.alloc([GH, 128], BF16)
            qTs = P.slot("qT")
            kT = A.alloc([GH, 128], BF16)
            kTs = P.slot("kT")
            att = A.alloc([GH, 128], BF16)
            atts = P.slot("att")
            G = A.alloc([GVW], F32)
            Gs = P.slot("G")
            og = A.alloc([GVW], BF16)
            ogs = P.slot("og")
            st = A.alloc([2 * GH], F32)
            sts = P.slot("st")
            junk = A.alloc([256], BF16)
            junks = P.slot("junk")
            tgw = min(512, S)
            ogst = [(A.alloc([GVW // 128, tgw], BF16), P.slot("ogst", dma=True)) for _ in range(2)]
        for tb in range(NB):
            r0 = tb * 128
            tmt, tms = tmb[tb % 2]
            P.dma("sp", tmt, tm1[r0:r0 + 128, :], writes=[tms])
            qv = tmt[:, 0:GQK]
            kv = tmt[:, GQK:2 * GQK]
            vv = tmt[:, 2 * GQK:2 * GQK + GVW]
            rv = tmt[:, 2 * GQK + GVW:W1]
            nh = (GQK + 511) // 512
            bz, bzs = BK.get(nh)
            for i in range(nh):
                cw = min(512, GQK - i * 512)
                mm(BK.f32(bz + i)[:, 0:cw], glw[0:17, r0:r0 + 128], wg[0:17, i * 512:i * 512 + cw], True, True, [gls, wgs], [bzs[i]], True)
            zps = BK.f32(bz, nh)[:, 0:GQK] if GQK % 512 == 0 else None
            for i in range(nh):
                cw = min(512, GQK - i * 512)
                act(e_[:, i * 512:i * 512 + cw], BK.f32(bz + i)[:, 0:cw], AF.Exp, [bzs[i]], [es_], scale=-1.0)
            act(l_, e_, AF.Ln, [es_], [ls_], bias=1.0)
            bd, bds = BK.get(nh)
            for i in range(nh):
                cw = min(512, GQK - i * 512)
                mm(BK.f32(bd + i)[:, 0:cw], cf[:, 2, :], l_[:, i * 512:i * 512 + cw], True, True, [cs, ls_], [bds[i]], True)
            for i in range(nh):
                cw = min(512, GQK - i * 512)
                act(ed[:, i * 512:i * 512 + cw], BK.f32(bd + i)[:, 0:cw], AF.Exp, [bds[i]], [eds])
            bt_, bts = BK.get(1)
            for h in range(GH):
                mm(BK.f32(bt_)[:, h:h + 1], l_[:, h * 128:(h + 1) * 128], cf[:, 1, 127:128], True, True, [ls_, cs], bts, h == GH - 1)
            act(dec, BK.f32(bt_)[:, 0:GH], AF.Exp, bts, [decs])
            tt("pool", kd, kv, ed, ALU.mult, [tms, eds], [kds])
            if not prescan:
                bb, bbs = BK.get(nh)
                for i in range(nh):
                    cw = min(512, GQK - i * 512)
                    mm(BK.f32(bb + i)[:, 0:cw], cf[:, 1, :], l_[:, i * 512:i * 512 + cw], True, True, [cs, ls_], [bbs[i]], True)
                for i in range(nh):
                    cw = min(512, GQK - i * 512)
                    act(eb[:, i * 512:i * 512 + cw], BK.f32(bb + i)[:, 0:cw], AF.Exp, [bbs[i]], [ebs])
                    act(enb[:, i * 512:i * 512 + cw], BK.f32(bb + i)[:, 0:cw], AF.Exp, [bbs[i]], [enbs], scale=-1.0)
                stt("dve", qt_, qv, 128 ** -0.5, eb, ALU.mult, ALU.mult, [tms, ebs], [qts])
                tt("dve", kt_, kv, enb, ALU.mult, [tms, enbs], [kts])
                hg = min(GH, 8)
                bq, bqs = BK.get(1)
                for h in range(GH):
                    tr(BK.bf(bq)[:, h * 128:(h + 1) * 128], qt_[:, h * 128:(h + 1) * 128], C["ident"], [qts, cs], bqs, h == GH - 1)
                cp("act", qT.rearrange("p a b -> p (a b)"), BK.bf(bq)[:, 0:GQK], bqs, [qTs])
                bk_, bks = BK.get(1)
                for h in range(GH):
                    tr(BK.bf(bk_)[:, h * 128:(h + 1) * 128], kt_[:, h * 128:(h + 1) * 128], C["ident"], [kts, cs], bks, h == GH - 1)
                cp("dve", kT.rearrange("p a b -> p (a b)"), BK.bf(bk_)[:, 0:GQK], bks, [kTs])
                for h0 in range(0, GH, 4):
                    hn_ = min(4, GH - h0)
                    ba, bas = BK.get(1)
                    for h in range(h0, h0 + hn_):
                        mm(BK.f32(ba)[:, (h - h0) * 128:(h - h0 + 1) * 128], kT[:, h, :], qT[:, h, :], True, True,
                           [kTs, qTs], bas, h == h0 + hn_ - 1)
                    for h in range(h0, h0 + hn_):
                        tt("dve", att[:, h, :], BK.f32(ba)[:, (h - h0) * 128:(h - h0 + 1) * 128], C["cmask_bf"], ALU.mult,
                           bas + [cs], [atts])
                act(G, rv, AF.Silu, [tms], [Gs])
                tt("pool", G, G, gg, ALU.mult, [Gs, ggs], [Gs])
                for h0 in range(0, GH, 2):
                    bo, bos = BK.get(1)
                    for h in range(h0, min(h0 + 2, GH)):
                        o_ps = BK.f32(bo)[:, (h - h0) * 256:(h - h0 + 1) * 256]
                        mm(o_ps, att[:, h, :], vv[:, h * 256:(h + 1) * 256], True, False, [atts, tms], bos, False)
                        mm(o_ps, qT[:, h, :], Sbf[:, h, :], False, True, [qTs, Sbs[h]], bos, True)
                    for h in range(h0, min(h0 + 2, GH)):
                        o_ps = BK.f32(bo)[:, (h - h0) * 256:(h - h0 + 1) * 256]
                        act(junk, o_ps, AF.Square, bos, [junks, sts], accum=st[:, h:h + 1])
                        ts("dve", st[:, GH + h:GH + h + 1], st[:, h:h + 1], 1.0 / 256, EPS, ALU.mult, ALU.add, [sts], [sts])
                        act(st[:, GH + h:GH + h + 1], st[:, GH + h:GH + h + 1], AF.Sqrt, [sts], [sts])
                        P.op("dve", lambda e, o=st[:, GH + h:GH + h + 1], i_=st[:, GH + h:GH + h + 1]: e.reciprocal(o, i_), reads=[sts], writes=[sts])
                        stt("dve", og[:, h * 256:(h + 1) * 256], o_ps, st[:, GH + h:GH + h + 1], G[:, h * 256:(h + 1) * 256],
                            ALU.mult, ALU.mult, bos + [sts, Gs], [ogs])
                ost, oss = ogst[(tb * 128 // tgw) % 2]
                tcol = (tb * 128) % tgw
                nchk = GVW // 128
                for c0 in range(0, nchk, 8):
                    cn = min(8, nchk - c0)
                    bt2, bt2s = BK.get(1)
                    for c in range(cn):
                        tr(BK.bf(bt2)[:, c * 128:(c + 1) * 128], og[:, (c0 + c) * 128:(c0 + c + 1) * 128], C["ident"], [ogs, cs], bt2s, c == cn - 1)
                    for c in range(cn):
                        evac(ost[:, c0 + c, tcol:tcol + 128], BK.bf(bt2)[:, c * 128:(c + 1) * 128], bt2s, [oss])
                if tcol + 128 == tgw:
                    t0 = tb * 128 + 128 - tgw
                    P.dma("sp", brT[0:nchk].rearrange("k p t -> p k t")[:, :, t0:t0 + tgw], ost, reads=[oss])
            for h0 in range(0, GH, 2):
                bu, bus = BK.get(1)
                for h in range(h0, min(h0 + 2, GH)):
                    mm(BK.f32(bu)[:, (h - h0) * 256:(h - h0 + 1) * 256], kd[:, h * 128:(h + 1) * 128], vv[:, h * 256:(h + 1) * 256],
                       True, True, [kds, tms], bus, h == min(h0 + 2, GH) - 1)
                for h in range(h0, min(h0 + 2, GH)):
                    stt("dve", Sst[:, h, :], Sst[:, h, :], dec[:, h:h + 1], BK.f32(bu)[:, (h - h0) * 256:(h - h0 + 1) * 256],
                        ALU.mult, ALU.add, [Ss[h], decs] + bus, [Ss[h]])
                    if not prescan:
                        cp("act", Sbf[:, h, :], Sst[:, h, :], [Ss[h]], [Sbs[h]])
        if prescan:
            P.dma("sp", gin_st, Sst.rearrange("p a b -> p (a b)"), reads=Ss, sem=Sld.sem)
            if "st" in dbg:
                P.dma("sp", dbg["st"], Sst.rearrange("p a b -> p (a b)"), reads=Ss, sem=Sld.sem)
        end_phase()

    if upto > 3:
        gla_scan(True)

    def collectives(pairs):
        begin_phase()
        X = P.E["pool"]
        for (i_, o_) in pairs:
            s = P.new_sem()
            s.count += 1
            X.items.append(("o", lambda e, a=i_, b=o_: e.collective_compute("AllGather", ALU.bypass, replica_groups=PAIRS,
                                                                            ins=[a.opt()], outs=[b.opt()]), s, 1))
            X.idx += 1
        end_phase()

    if upto > 4:
        collectives([(gin_fk[i * fk_rows:(i + 1) * fk_rows, :], gout_fk[i]) for i in range(len(gout_fk))]
                    + [(gin_fv[i * fv_rows:(i + 1) * fv_rows, :], gout_fv[i]) for i in range(len(gout_fv))]
                    + [(gin_rs, gout_rs), (gin_st, gout_st)])

    if upto > 5:
        gla_scan(False)

    def fox_attn():
        begin_phase()
        C = load_consts(["cmask_bf", "ones_bf", "f32", "flags"])
        cs = C["slot"]
        cf = C["f32"]
        l = A.alloc([NB, FH], F32)
        ls = P.slot("l", dma=True)
        P.dma("sp", l, ffd.rearrange("(n p) h -> p n h", p=128), writes=[ls])
        bf_ = A.alloc([FH], F32)
        bfs = P.slot("bf", dma=True)
        P.dma("sp", bf_, bfox.partition_broadcast(128), writes=[bfs])
        for n in range(NB):
            tt("dve", l[:, n, :], l[:, n, :], bf_, ALU.add, [ls, bfs], [ls])
        l2 = l.rearrange("p n h -> p (n h)")
        act(l2, l2, AF.Exp, [ls], [ls], scale=-1.0)
        act(l2, l2, AF.Ln, [ls], [ls], bias=1.0)
        NF = NB * FH
        b, bs = BK.get(1)
        mm(BK.f32(b)[:, 0:NF], cf[:, 3, :], l2, True, True, [cs, ls], bs, True)
        b2, bs2 = BK.get(1)
        mm(BK.f32(b2)[:, 0:NF], cf[:, 4, :], l2, True, True, [cs, ls], bs2, True)
        tot = A.alloc([NB, FH], F32)
        tots = P.slot("tot")
        cp("dve", tot.rearrange("p n h -> p (n h)"), BK.f32(b2)[:, 0:NF], bs2, [tots])
        off = A.alloc([NB + 1, FH], F32)
        offs = P.slot("off")
        memset("dve", off[:, 0, :], 0.0, [offs])
        for n in range(NB):
            tt("dve", off[:, n + 1, :], off[:, n, :], tot[:, n, :], ALU.add, [offs, tots], [offs])
        kb = A.alloc([2 * NB, FH], F32)
        kbs = P.slot("kb", dma=True)
        P.dma("sp", kb[:, 0:NB, :].rearrange("p n h -> p (n h)"), gout_rs[0:128, :], writes=[kbs])
        ts("dve", kb[:, 0:NB, :].rearrange("p n h -> p (n h)"), kb[:, 0:NB, :].rearrange("p n h -> p (n h)"),
           C["flags"][:, 1:2], None, ALU.add, None, [kbs, cs], [kbs])
        tt("dve", kb[:, NB:2 * NB, :].rearrange("p n h -> p (n h)"), BK.f32(b)[:, 0:NF],
           off[:, 0:NB, :].rearrange("p n h -> p (n h)"), ALU.add, bs + [offs], [kbs])
        Kb = [(A.alloc([2 * S], BF16), P.slot("K", dma=True)) for _ in range(2)]
        Vb = [(A.alloc([2 * NB, 128], BF16), P.slot("V", dma=True)) for _ in range(2)]
        Qb = [(A.alloc([S], BF16), P.slot("Q", dma=True)) for _ in range(2)]
        Ob = [(A.alloc([S], BF16), P.slot("O", dma=True)) for _ in range(2)]
        pTb = [(A.alloc([512], BF16), P.slot("pT")) for _ in range(4)]
        crb = [(A.alloc([S], BF16, parts=32), P.slot("crow")) for _ in range(2)]
        rinv = A.alloc([512], F32)
        rinvs = P.slot("rinv")
        NG4 = NB // 4
        ctr = dict(p=0, q=0, s=0)
        steps = []

        def head_loads(h):
            Kt, Ks = Kb[h % 2]
            Vt, Vs = Vb[h % 2]
            Qt, Qs = Qb[h % 2]
            kr = (h * 128) % fk_rows
            P.dma("sp", Kt[:, 0:S], gout_fk[(h * 128) // fk_rows][kr:kr + 128, :], writes=[Ks])
            P.dma("sp", Kt[:, S:2 * S], gin_fk[h * 128:(h + 1) * 128, :], writes=[Ks])
            for vi in range(len(gout_fv)):
                nbv = fv_rows // 128
                P.dma("sp", Vt[:, vi * nbv:(vi + 1) * nbv, :],
                      gout_fv[vi][0:fv_rows, h * 128:(h + 1) * 128].rearrange("(n p) d -> p n d", p=128), writes=[Vs])
            P.dma("sp", Vt[:, NB:2 * NB, :], gin_fv[:, h * 128:(h + 1) * 128].rearrange("(n p) d -> p n d", p=128), writes=[Vs])
            P.dma("sp", Qt, fqT[h * 128:(h + 1) * 128, :], writes=[Qs])
            cr, crs = crb[h % 2]
            for i in range(NB):
                ts("dve", cr[0:1, i * 128:(i + 1) * 128], cf[0:1, 4, :], off[0:1, i + 1, h:h + 1], -1.0, ALU.mult, ALU.mult,
                   [cs, offs], [crs])

        for h in range(FH):
            for g in range(NG4):
                nwide = NB + 4 * g
                nsteps = nwide + 4
                for si in range(nsteps):
                    st_ = {}

                    def S_fn(h=h, g=g, si=si, nwide=nwide, st_=st_):
                        if g == 0 and si == 0:
                            head_loads(h)
                        Kt, Ks = Kb[h % 2]
                        Qt, Qs = Qb[h % 2]
                        cr, crs = crb[h % 2]
                        bsx = 4 + ctr["s"] % 4
                        ctr["s"] += 1
                        bsxs = [BK.slots[bsx]]
                        st_["bsx"], st_["bsxs"] = bsx, bsxs
                        if si < nwide:
                            j = si
                            mm(BK.f32(bsx), Kt[:, j * 128:(j + 1) * 128], Qt[:, g * 512:(g + 1) * 512], True, False, [Ks, Qs], bsxs, False)
                            mm(BK.f32(bsx), C["ones_bf"][0:1, :], cr[0:1, g * 512:(g + 1) * 512], False, True, [cs, crs], bsxs, True)
                        else:
                            ii = si - nwide
                            i = 4 * g + ii
                            for jj in range(ii + 1):
                                j = nwide + jj
                                o_ = BK.f32(bsx)[:, jj * 128:(jj + 1) * 128]
                                mm(o_, Kt[:, j * 128:(j + 1) * 128], Qt[:, i * 128:(i + 1) * 128], True, False, [Ks, Qs], bsxs, False)
                                mm(o_, C["ones_bf"][0:1, :], cr[0:1, i * 128:(i + 1) * 128], False, True, [cs, crs], bsxs, jj == ii)

                    def EXP_fn(h=h, g=g, si=si, nwide=nwide, st_=st_):
                        bsx, bsxs = st_["bsx"], st_["bsxs"]
                        pT, pTs = pTb[ctr["p"] % 4]
                        ctr["p"] += 1
                        st_["pT"], st_["pTs"] = pT, pTs
                        if si < nwide:
                            act(pT, BK.f32(bsx), AF.Exp, bsxs + [kbs], [pTs], bias=kb[:, si, h:h + 1])
                        else:
                            ii = si - nwide
                            for jj in range(ii + 1):
                                j = nwide + jj
                                act(pT[:, jj * 128:(jj + 1) * 128], BK.f32(bsx)[:, jj * 128:(jj + 1) * 128], AF.Exp, bsxs + [kbs], [pTs],
                                    bias=kb[:, j, h:h + 1])
                            tt("dve", pT[:, ii * 128:(ii + 1) * 128], pT[:, ii * 128:(ii + 1) * 128], C["cmask_bf"], ALU.mult, [pTs, cs], [pTs])

                    def PV_fn(h=h, g=g, si=si, nwide=nwide, nsteps=nsteps, st_=st_):
                        Vt, Vs = Vb[h % 2]
                        Ot, Os = Ob[h % 2]
                        pT, pTs = st_["pT"], st_["pTs"]
                        if si == 0:
                            ctr["q"] += 1
                        bo = 2 * (ctr["q"] % 2)
                        br = bo + 1
                        bos, brs = [BK.slots[bo]], [BK.slots[br]]
                        o_ps = BK.f32(bo)
                        r_ps = BK.f32(br)
                        if si < nwide:
                            j = si
                            mm(o_ps, Vt[:, j, :], pT, j == 0, False, [Vs, pTs], bos, False)
                            mm(r_ps, C["ones_bf"], pT, j == 0, False, [cs, pTs], brs, False)
                        else:
                            ii = si - nwide
                            for jj in range(ii + 1):
                                j = nwide + jj
                                lastq = (jj == ii)
                                mm(o_ps[:, ii * 128:(ii + 1) * 128], Vt[:, j, :], pT[:, jj * 128:(jj + 1) * 128], False, lastq, [Vs, pTs], bos,
                                   lastq and ii == 3, skip=True)
                                mm(r_ps[:, ii * 128:(ii + 1) * 128], C["ones_bf"], pT[:, jj * 128:(jj + 1) * 128], False, lastq, [cs, pTs], brs,
                                   lastq and ii == 3, skip=True)
                        if si == nsteps - 1:
                            P.op("dve", lambda e, o=rinv, i_=r_ps: e.reciprocal(o, i_), reads=brs, writes=[rinvs])
                            tt("dve", Ot[:, g * 512:(g + 1) * 512], o_ps, rinv, ALU.mult, bos + [rinvs], [Os])
                            if g == NG4 - 1:
                                P.dma("sp", brT[GVW // 128 + h], Ot, reads=[Os])
                    steps.append((S_fn, EXP_fn, PV_fn))
        for k in range(len(steps) + 1):
            if k < len(steps):
                steps[k][0]()
                steps[k][1]()
            if k >= 1:
                steps[k - 1][2]()
        end_phase()

    if upto > 6:
        fox_attn()

    def g2():
        NG = GVW // 128
        st = {}

        def setup():
            st["g"] = [(A.alloc([2, 512], BF16), P.slot("g", dma=True)) for _ in range(2)]
            st["tmp"] = A.alloc([512], F32)
            st["tmps"] = P.slot("tmp")
            st["out"] = Stage(2, [512], BF16, "mo")
            st["i"] = 0
            return {}

        def tiles_for(cblk):
            def f(tti, t0, wt, ws, ctx):
                at = ctx["at"]
                res = []
                tgw = min(512, T)
                for sub in range(4):
                    ch = cblk * 4 + sub
                    for tg in range(T // tgw):
                        ml = [(0, (lambda b, w=tgw: BK.f32(b)[:, 0:w]), wt[:, k, sub * 128:(sub + 1) * 128], at[:, k, tg * tgw:(tg + 1) * tgw]) for k in range(NG)]
                        ml += [(1, (lambda b, w=tgw: BK.f32(b)[:, 0:w]), wt[:, NG + k, sub * 128:(sub + 1) * 128], at[:, NG + k, tg * tgw:(tg + 1) * tgw]) for k in range(FH)]

                        def epi(b, bs, ch=ch, tt0=t0 + tg * tgw, w=tgw):
                            gt, gs = st["g"][st["i"] % 2]
                            st["i"] += 1
                            P.dma("sp", gt[:, 0, 0:w], sgT[ch, :, tt0:tt0 + w], writes=[gs])
                            P.dma("sp", gt[:, 1, 0:w], sgT[KC + ch, :, tt0:tt0 + w], writes=[gs])
                            so, sos = st["out"].next()
                            tt("dve", st["tmp"][:, 0:w], BK.f32(b)[:, 0:w], gt[:, 0, 0:w], ALU.mult, [bs[0], gs], [st["tmps"]])
                            tt("dve", so[:, 0:w], BK.f32(b + 1)[:, 0:w], gt[:, 1, 0:w], ALU.mult, [bs[1], gs], [sos])
                            tt("dve", so[:, 0:w], so[:, 0:w], st["tmp"][:, 0:w], ALU.add, [sos, st["tmps"]], [sos])
                            P.dma("sp", mT[ch, :, tt0:tt0 + w], so[:, 0:w], reads=[sos])
                        res.append((ml, epi))
                return res
            return f

        blocks = []
        for cb in range(D // 512):
            blocks.append(dict(parts=[(w_gb, 0, NG, cb * 512, 512, 0, 0), (w_fb, 0, FH, cb * 512, 512, NG, 0)], tiles=tiles_for(cb)))
        gemm_phase(brT, BRC, S, T, blocks, extra_setup=setup)

    if upto > 7:
        g2()

    def tm_gemm_to_y(a_src, akc, W, Tt, kparts=1):
        st = {}

        def setup():
            st["s"] = Stage(3, [512], F32, "ys")
            return {}

        def y_store(c0):
            def epi(b, bs, r0):
                s_, ss = st["s"].next()
                evac(s_, BK.f32(b), bs, [ss])
                P.dma("sp", yb[r0:r0 + 128, c0:c0 + 512], s_, reads=[ss])
            return epi
        blocks = []
        if kparts == 1:
            for c0 in range(0, D, 512):
                blocks.append(dict(parts=[(W, 0, akc, c0, 512, 0, 0)], tiles=tm_tiles(Tt, akc, 512, 0, y_store(c0))))
            gemm_phase(a_src, akc, S, Tt, blocks, extra_setup=setup)
        else:
            kh = (akc + 1) // 2
            ntb = Tt // 128
            begin_phase()
            setup()
            at = A.alloc([akc, Tt], BF16)
            ats = P.slot("A", dma=True)
            wb = [(A.alloc([kh, 512], BF16), P.slot(f"w{i}", dma=True)) for i in range(2)]
            items = [(tti, c0, half) for tti in range(S // Tt) for c0 in range(0, D, 512) for half in range(2)]
            Wv = W.rearrange("(k p) n -> p k n", p=128)

            def issue(k):
                wt, ws = wb[k % 2]
                _, c0, half = items[k]
                k0, nk = (0, kh) if half == 0 else (kh, akc - kh)
                for kk in range(0, nk, 8):
                    kn = min(8, nk - kk)
                    P.dma("pool", wt[:, kk:kk + kn, :], Wv[:, k0 + kk:k0 + kk + kn, c0:c0 + 512], writes=[ws])

            issue(0)
            cur_tt = -1
            bb = bbs = None
            for k, (tti, c0, half) in enumerate(items):
                t0 = tti * Tt
                if tti != cur_tt:
                    cur_tt = tti
                    for k0 in range(0, akc, 8):
                        kn = min(8, akc - k0)
                        P.dma("sp", at[:, k0:k0 + kn, :], a_src.rearrange("k p t -> p k t")[:, k0:k0 + kn, t0:t0 + Tt], writes=[ats])
                if k + 1 < len(items):
                    issue(k + 1)
                wt, ws = wb[k % 2]
                if half == 0:
                    bb, bbs = BK.get(ntb)
                k0, nk = (0, kh) if half == 0 else (kh, akc - kh)
                for tb in range(ntb):
                    for kq in range(nk):
                        last = (half == 1 and kq == nk - 1)
                        mm(BK.f32(bb + tb), at[:, k0 + kq, tb * 128:(tb + 1) * 128], wt[:, kq, :], half == 0 and kq == 0, last,
                           [ats, ws], [bbs[tb]], last or kq == nk - 1)
                if half == 1:
                    for tb in range(ntb):
                        y_store(c0)(bb + tb, [bbs[tb]], t0 + tb * 128)
            end_phase()

    if upto > 8:
        tm_gemm_to_y(mT, KC, w_out, T)

    if upto > 9:
        nr_phase(x, S, yb, 0, 1, out, hT)

    if upto > 10:
        nr_phase(mem, MEM, None, 0, 2, None, memT)

    def xa_kv():
        st = {}

        def setup():
            st["fm"] = Stage(2, [512], BF16, "sfm")
            st["tm"] = Stage(2, [512], BF16, "stm")
            return {}

        def k_store(c0):
            def epi(b, bs, sub, tt0, cw, w):
                s_, ss = st["fm"].next()
                evac(s_[:, 0:w], BK.f32(b)[:, 0:w], bs, [ss])
                P.dma("sp", kxT[c0 // 128 + sub, :, tt0:tt0 + w], s_[:, 0:w], reads=[ss])
            return epi

        def v_store(c0):
            def epi(b, bs, r0):
                s_, ss = st["tm"].next()
                evac(s_, BK.f32(b), bs, [ss])
                P.dma("sp", vx[r0:r0 + 128, c0:c0 + 512], s_, reads=[ss])
            return epi
        blocks = []
        for c0 in range(0, XW, 512):
            blocks.append(dict(parts=[(w_xkv, 0, KC, c0, 512, 0, 0)], tiles=fm_tiles(MEM, KC, 512, 0, k_store(c0))))
        for c0 in range(0, XW, 512):
            blocks.append(dict(parts=[(w_xkv, 0, KC, XW + c0, 512, 0, 0)], tiles=tm_tiles(MEM, KC, 512, 0, v_store(c0))))
        gemm_phase(memT, KC, MEM, MEM, blocks, extra_setup=setup)

    if upto > 11:
        xa_kv()

    def xa_q():
        st = {}

        def setup():
            st["fm"] = Stage(3, [512], BF16, "sfm")
            return {}

        def q_store(c0):
            def epi(b, bs, sub, tt0, cw, w):
                s_, ss = st["fm"].next()
                evac(s_[:, 0:w], BK.f32(b)[:, 0:w], bs, [ss], scale=256 ** -0.5)
                P.dma("sp", qxT[c0 // 128 + sub, :, tt0:tt0 + w], s_[:, 0:w], reads=[ss])
            return epi
        blocks = []
        for c0 in range(0, XW, 512):
            cw = min(512, XW - c0)
            blocks.append(dict(parts=[(w_xq, 0, KC, c0, cw, 0, 0)], tiles=fm_tiles(T, KC, cw, 0, q_store(c0))))
        gemm_phase(hT, KC, S, T, blocks, extra_setup=setup)

    if upto > 12:
        xa_q()

    def xa_attn():
        begin_phase()
        C = load_consts(["ones_bf"])
        cs = C["slot"]
        XC = XW // 128
        MB = MEM // 128
        kt = A.alloc([XC, MEM], BF16)
        kts = P.slot("kx", dma=True)
        P.dma("sp", kt, kxT.rearrange("k p t -> p k t"), writes=[kts])
        vt = A.alloc([MB, XW], BF16)
        vts = P.slot("vx", dma=True)
        P.dma("sp", vt, vx.rearrange("(n p) d -> p n d", p=128), writes=[vts])
        tgw = min(512, S)
        qb = [(A.alloc([XC, tgw], BF16), P.slot("q", dma=True)) for _ in range(2)]
        ob = [(A.alloc([XC, tgw], BF16), P.slot("o", dma=True)) for _ in range(2)]
        pb = [(A.alloc([tgw], BF16), P.slot("p")) for _ in range(3)]
        rinv = A.alloc([tgw], F32)
        rinvs = P.slot("rinv")
        pi = 0
        for tg in range(S // tgw):
            t0 = tg * tgw
            qt, qs = qb[tg % 2]
            ot, os_ = ob[tg % 2]
            P.dma("sp", qt, qxT.rearrange("k p t -> p k t")[:, :, t0:t0 + tgw], writes=[qs])
            for h in range(XH):
                bo0, bo0s = BK.get(1)
                bo1, bo1s = BK.get(1)
                br, brs = BK.get(1)
                for m in range(MB):
                    bsx, bsxs = BK.get(1)
                    for c in range(2):
                        mm(BK.f32(bsx)[:, 0:tgw], kt[:, 2 * h + c, m * 128:(m + 1) * 128], qt[:, 2 * h + c, :], c == 0, c == 1,
                           [kts, qs], bsxs, c == 1)
                    pt, ps_ = pb[pi % 3]
                    pi += 1
                    act(pt, BK.f32(bsx)[:, 0:tgw], AF.Exp, bsxs, [ps_])
                    mm(BK.f32(bo0)[:, 0:tgw], vt[:, m, h * 256:h * 256 + 128], pt, m == 0, m == MB - 1, [vts, ps_], bo0s, m == MB - 1)
                    mm(BK.f32(bo1)[:, 0:tgw], vt[:, m, h * 256 + 128:h * 256 + 256], pt, m == 0, m == MB - 1, [vts, ps_], bo1s, m == MB - 1)
                    mm(BK.f32(br)[:, 0:tgw], C["ones_bf"], pt, m == 0, m == MB - 1, [cs, ps_], brs, m == MB - 1)
                P.op("dve", lambda e, o=rinv, i_=BK.f32(br)[:, 0:tgw]: e.reciprocal(o, i_), reads=brs, writes=[rinvs])
                tt("dve", ot[:, 2 * h, :], BK.f32(bo0)[:, 0:tgw], rinv, ALU.mult, bo0s + [rinvs], [os_])
                tt("dve", ot[:, 2 * h + 1, :], BK.f32(bo1)[:, 0:tgw], rinv, ALU.mult, bo1s + [rinvs], [os_])
            P.dma("sp", oxT.rearrange("k p t -> p k t")[:, :, t0:t0 + tgw], ot, reads=[os_])
        end_phase()

    if upto > 13:
        xa_attn()
    if upto > 14:
        tm_gemm_to_y(oxT, XW // 128, w_xo, T)
    if upto > 15:
        nr_phase(out, S, yb, 1, 3, out, hT, halo=True)
    if upto > 16:
        collectives([(gin_hl, gout_hl)])

    def g6():
        st = {}

        def setup():
            C = load_consts(["flags"])
            st["C"] = C
            st["wc"] = A.alloc([3, 2 * FC], F32)
            st["wcs"] = P.slot("wc", dma=True)
            P.dma("sp", st["wc"], wconv.rearrange("p (a b) -> p a b", a=3), writes=[st["wcs"]])
            st["bc"] = A.alloc([2 * FC], F32)
            P.dma("sp", st["bc"], bconv, writes=[st["wcs"]])
            st["hstore"] = A.alloc([2 * FC, 2], F32)
            st["hss"] = P.slot("hstore")
            st["hh"] = A.alloc([KC, 2], BF16)
            st["hhs"] = P.slot("hh", dma=True)
            P.dma("sp", st["hh"].rearrange("p a b -> p (a b)"), gout_hl[0:128, :], writes=[st["hhs"]])
            st["ue"] = [(A.alloc([2, 520], F32), P.slot("ue")) for _ in range(2)]
            st["cg"] = A.alloc([512], F32)
            st["cgs"] = P.slot("cg")
            st["cu"] = A.alloc([512], F32)
            st["cus"] = P.slot("cu")
            st["t1"] = A.alloc([512], F32)
            st["t1s"] = P.slot("t1")
            st["sg"] = A.alloc([512], F32)
            st["sgs"] = P.slot("sg")
            st["out"] = Stage(2, [512], BF16, "ao")
            st["i"] = 0
            return {}

        def tiles_for(c0, cw):
            def f(tti, t0, wt, ws, ctx):
                at = ctx["at"]
                res = []
                tgw = min(512, T)
                for sub in range(cw // 128):
                    ci = c0 // 128 + sub
                    if tti == 0:
                        ml = [(0, (lambda b: BK.f32(b)[:, 0:2]), wt[:, k, sub * 128:(sub + 1) * 128], st["hh"][:, k, :]) for k in range(KC)]
                        ml += [(1, (lambda b: BK.f32(b)[:, 0:2]), wt[:, k, 256 + sub * 128:256 + (sub + 1) * 128], st["hh"][:, k, :]) for k in range(KC)]

                        def epi_h(b, bs, ci=ci):
                            fl = st["C"]["flags"][:, 0:1]
                            ts("dve", st["hstore"][:, ci, :], BK.f32(b)[:, 0:2], fl, None, ALU.mult, None, [bs[0], st["C"]["slot"]], [st["hss"]])
                            ts("dve", st["hstore"][:, FC + ci, :], BK.f32(b + 1)[:, 0:2], fl, None, ALU.mult, None, [bs[1], st["C"]["slot"]], [st["hss"]])
                        res.append((ml, epi_h, [st["hhs"]]))
                    for tg in range(T // tgw):
                        ml = [(0, (lambda b, w=tgw: BK.f32(b)[:, 0:w]), wt[:, k, sub * 128:(sub + 1) * 128], at[:, k, tg * tgw:(tg + 1) * tgw]) for k in range(KC)]
                        ml += [(1, (lambda b, w=tgw: BK.f32(b)[:, 0:w]), wt[:, k, 256 + sub * 128:256 + (sub + 1) * 128], at[:, k, tg * tgw:(tg + 1) * tgw]) for k in range(KC)]

                        def epi(b, bs, ci=ci, tt0=t0 + tg * tgw, w=tgw):
                            ue, ues = st["ue"][st["i"] % 2]
                            st["i"] += 1
                            wc, bc, wcs = st["wc"], st["bc"], st["wcs"]
                            outs = []
                            for part, (cidx, dst, dsts) in enumerate(((ci, st["cg"], st["cgs"]), (FC + ci, st["cu"], st["cus"]))):
                                u = ue[:, part, :]
                                cp("dve", u[:, 0:2], st["hstore"][:, cidx, :], [st["hss"]], [ues])
                                cp("act", u[:, 2:2 + w], BK.f32(b + part)[:, 0:w], [bs[part]], [ues])
                                cp("dve", st["hstore"][:, cidx, :], u[:, w:w + 2], [ues], [st["hss"]])
                                act(dst[:, 0:w], u[:, 2:2 + w], AF.Identity, [ues, wcs], [dsts], bias=bc[:, cidx:cidx + 1], scale=wc[:, 2, cidx:cidx + 1])
                                stt("dve", dst[:, 0:w], u[:, 1:1 + w], wc[:, 1, cidx:cidx + 1], dst[:, 0:w], ALU.mult, ALU.add, [ues, wcs, dsts], [dsts])
                                stt("dve", dst[:, 0:w], u[:, 0:w], wc[:, 0, cidx:cidx + 1], dst[:, 0:w], ALU.mult, ALU.add, [ues, wcs, dsts], [dsts])
                            cg, cu, t1, sg = st["cg"], st["cu"], st["t1"], st["sg"]
                            tt("dve", t1[:, 0:w], cg[:, 0:w], cg[:, 0:w], ALU.mult, [st["cgs"]], [st["t1s"]])
                            ts("dve", t1[:, 0:w], t1[:, 0:w], 0.044715, 1.0, ALU.mult, ALU.add, [st["t1s"]], [st["t1s"]])
                            tt("dve", t1[:, 0:w], t1[:, 0:w], cg[:, 0:w], ALU.mult, [st["t1s"], st["cgs"]], [st["t1s"]])
                            act(sg[:, 0:w], t1[:, 0:w], AF.Sigmoid, [st["t1s"]], [st["sgs"]], scale=1.5957691216057308)
                            tt("dve", sg[:, 0:w], sg[:, 0:w], cg[:, 0:w], ALU.mult, [st["sgs"], st["cgs"]], [st["sgs"]])
                            so, sos = st["out"].next()
                            tt("dve", so[:, 0:w], sg[:, 0:w], cu[:, 0:w], ALU.mult, [st["sgs"], st["cus"]], [sos])
                            P.dma("sp", aT[ci, :, tt0:tt0 + w], so[:, 0:w], reads=[sos])
                        res.append((ml, epi))
                return res
            return f

        blocks = []
        for c0 in range(0, DFF, 256):
            cw = min(256, DFF - c0)
            blocks.append(dict(parts=[(w_up, 0, KC, c0, cw, 0, 0), (w_up, 0, KC, DFF + c0, cw, 0, 256)], tiles=tiles_for(c0, cw)))
        gemm_phase(hT, KC, S, T, blocks, extra_setup=setup, nwbuf=2)

    if upto > 17:
        g6()
    if upto > 18:
        tm_gemm_to_y(aT, FC, w_dn, min(512, S), kparts=2)
    if upto > 19:
        nr_phase(out, S, yb, 2, 0, out, None)

    with nc.Block() as block:
        P.emit(block)
    es.close()
    _NC_CACHE["P"] = P
    return nc


def make_consts():
    s = np.arange(128)[:, None]
    t = np.arange(128)[None, :]
    ident = (s == t).astype(np.float32)
    triu = (s <= t).astype(np.float32)
    tril = (s > t).astype(np.float32)
    ones = np.ones((128, 128), np.float32)
    return np.ascontiguousarray(np.concatenate([ident, triu * (-1.0 / 16), tril * (-1.0 / 16), triu, ones], axis=1))


def make_in_maps(cfg, inp, n_cores=8):
    D, S, KC, FC, GH = cfg["D"], cfg["S"], cfg["KC"], cfg["FC"], cfg["GH"]
    f = lambda a: np.ascontiguousarray(np.asarray(a, dtype=np.float32))
    col = lambda v: f(np.asarray(v).reshape(-1, 128).T)
    L0 = lambda k: np.asarray(inp[k])[0]
    shared = {
        "w_in": f(L0("w_in")), "w_gla_branch": f(L0("w_gla_branch")), "w_fox_branch": f(L0("w_fox_branch")),
        "w_out": f(L0("w_out")), "w_xa_q": f(L0("w_xa_q")), "w_xa_kv": f(L0("w_xa_kv")), "w_xa_o": f(L0("w_xa_o")),
        "w_ffn_up": f(L0("w_ffn_up")), "w_ffn_down": f(L0("w_ffn_down")),
        "gcols": f(np.concatenate([col(L0("g_mix_pre")), col(L0("g_xa_pre")), col(L0("g_mem")), col(L0("g_ffn_pre"))], axis=1)),
        "gpost": f(np.stack([L0("g_mix_post"), L0("g_xa_post"), L0("g_ffn_post")])),
        "bgate": col(L0("b_gate")),
        "wgu": f(np.concatenate([L0("w_gla_gate_up"), L0("b_gla_gate")[None, :]], axis=0)),
        "ggla": f(np.tile(L0("g_gla_norm"), GH)[None, :]),
        "bfox": f(L0("b_fox_f")[None, :]),
        "wconv": f(np.stack([col(L0("w_conv")[j]) for j in range(3)], axis=1).reshape(128, -1)),
        "bconv": col(L0("b_conv")),
        "consts": make_consts(),
    }
    xs = np.asarray(inp["x"])
    ms = np.asarray(inp["mem"])
    maps = []
    for c in range(n_cores):
        b, half = c // 2, c % 2
        m = dict(shared)
        m["x"] = f(xs[b, half * S:(half + 1) * S])
        m["mem"] = f(ms[b])
        fl = np.zeros((128, 2), np.float32)
        fl[:, 0] = float(half)
        fl[:, 1] = (float(half) - 1.0) * (-NEG_BIG)
        m["flags"] = fl
        maps.append(m)
    return maps


_NC_CACHE = {}


def kernel(**inputs):
    cfg = make_cfg()
    if "nc" not in _NC_CACHE:
        _NC_CACHE["nc"] = build_program(cfg)
    nc = _NC_CACHE["nc"]
    maps = make_in_maps(cfg, inputs)
    res = run_bass_kernel_spmd(nc, maps, core_ids=list(range(8)))
    B = np.asarray(inputs["x"]).shape[0]
    S = cfg["S"]
    outp = np.empty((B, 2 * S, cfg["D"]), np.float32)
    for c in range(8):
        outp[c // 2, (c % 2) * S:(c % 2 + 1) * S] = res.results[c]["out"]
    return outp
```

```python
import bisect
from contextlib import ExitStack
import numpy as np
import concourse.bass as bass
import concourse.mybir as mybir
from concourse.bass_utils import run_bass_kernel_spmd

F32 = mybir.dt.float32
BF16 = mybir.dt.bfloat16
AF = mybir.ActivationFunctionType
ALU = mybir.AluOpType
EPS = 1e-6
NEG_BIG = -30000.0


def make_cfg(D=4096, S=2048, GH=8, FH=16, XH=4, MEM=256, DFF=11008):
    c = dict(D=D, S=S, GH=GH, FH=FH, XH=XH, MEM=MEM, DFF=DFF)
    c["KC"] = D // 128
    c["NB"] = S // 128
    c["GQK"] = GH * 128
    c["GVW"] = GH * 256
    c["FW"] = FH * 128
    c["XW"] = XH * 256
    c["FC"] = DFF // 128
    c["IN_COLS"] = 2 * c["GQK"] + 2 * c["GVW"] + 16 + 3 * c["FW"] + FH + 2 * D
    c["T"] = min(1024, S)
    return c


class Sem:
    def __init__(self, h):
        self.h = h
        self.count = 0


class EngRec:
    def __init__(self, name, sem):
        self.name = name
        self.sem = sem
        self.items = []
        self.idx = 0
        self.sig_idx = []
        self.sig_val = []
        self.seen = {}


class Slot:
    __slots__ = ("name", "w", "r", "sem")

    def __init__(self, name, sem=None):
        self.name = name
        self.w = {}
        self.r = {}
        self.sem = sem


class Prog:
    def __init__(self, nc, sems):
        self.nc = nc
        self.pool = [Sem(h) for h in sems]
        self.E = {n: EngRec(n, self.pool.pop()) for n in ("pe", "act", "dve", "pool", "sp")}
        self.phase_sems = []
        self.used_dma_sems = []

    def new_sem(self):
        s = self.pool.pop()
        self.phase_sems.append(s)
        return s

    def slot(self, name, dma=False):
        return Slot(name, self.new_sem() if dma else None)

    def _resolve(self, tok):
        if tok[0] == "s":
            return tok[1], tok[2]
        e, idx = tok[1], tok[2]
        k = bisect.bisect_left(e.sig_idx, idx)
        assert k < len(e.sig_idx), f"no signal after idx {idx} on {e.name}"
        return e.sem, e.sig_val[k]

    def _wait(self, X, tok):
        if tok is None:
            return
        if tok[0] == "e" and tok[1] is X and X.name == "pe":
            return
        sem, val = self._resolve(tok)
        if X.seen.get(sem, 0) >= val:
            return
        X.seen[sem] = val
        X.items.append(("w", sem, val))

    def _deps(self, X, reads, writes, disjoint=False):
        for s in reads:
            for t in s.w.values():
                self._wait(X, t)
        for s in writes:
            for k, t in s.w.items():
                if disjoint and k == X.name:
                    continue
                self._wait(X, t)
            for t in s.r.values():
                self._wait(X, t)

    def op(self, e, fn, reads=(), writes=(), signal=True, disjoint=False):
        X = self.E[e]
        self._deps(X, reads, writes, disjoint)
        tok = ("e", X, X.idx)
        if signal:
            X.sem.count += 1
            X.sig_idx.append(X.idx)
            X.sig_val.append(X.sem.count)
            X.items.append(("o", fn, X.sem, 1))
        else:
            X.items.append(("o", fn, None, 0))
        X.idx += 1
        for s in reads:
            s.r[e] = tok
        for s in writes:
            s.w[e] = tok
            s.r = {}
        return tok

    def dma(self, q, out, in_, reads=(), writes=(), sem=None):
        X = self.E[q]
        if sem is None:
            sem = (writes[0] if writes else reads[0]).sem
        assert sem is not None
        skip = ("d", id(sem))
        for s in reads:
            for t in s.w.values():
                self._wait(X, t)
        for s in writes:
            for k, t in s.w.items():
                if k != skip:
                    self._wait(X, t)
            for t in s.r.values():
                self._wait(X, t)
        sem.count += 16
        tok = ("s", sem, sem.count)
        X.items.append(("o", lambda eng, o=out, i=in_: eng.dma_start(out=o, in_=i), sem, 16))
        X.idx += 1
        for s in reads:
            s.r[("d", id(sem))] = tok
        for s in writes:
            s.w[("d", id(sem))] = tok
            s.r = {}
        return tok

    def barrier(self):
        toks = []
        for n in ("pe", "act", "dve", "pool"):
            e = self.E[n]
            if e.sig_idx:
                toks.append(("s", e.sem, e.sig_val[-1]))
        for s in self.phase_sems:
            if s.count:
                toks.append(("s", s, s.count))
        for n in ("pe", "act", "dve", "pool", "sp"):
            X = self.E[n]
            for t in toks:
                sem, val = t[1], t[2]
                if X.seen.get(sem, 0) >= val:
                    continue
                X.seen[sem] = val
                X.items.append(("w", sem, val))
        self.pool.extend(self.phase_sems)
        self.phase_sems = []

    def emit(self, block):
        def run(X):
            def f(eng):
                for it in X.items:
                    if it[0] == "w":
                        eng.wait_ge(it[1].h, it[2])
                    else:
                        ins = it[1](eng)
                        if it[2] is not None:
                            ins.then_inc(it[2].h, it[3])
            return f
        block.tensor(run(self.E["pe"]))
        block.scalar(run(self.E["act"]))
        block.vector(run(self.E["dve"]))
        block.gpsimd(run(self.E["pool"]))
        block.sync(run(self.E["sp"]))


class Arena:
    def __init__(self, ap, words):
        self.ap = ap
        self.words = words
        self.off = 0

    def reset(self):
        self.off = 0

    def alloc(self, shape, dt, parts=128):
        n = int(np.prod(shape))
        sz = 2 if dt == BF16 else 4
        w = (n * sz + 3) // 4
        w = (w + 7) // 8 * 8
        assert self.off + w <= self.words, f"SBUF arena overflow {self.off + w} > {self.words}"
        a = self.ap[0:parts, self.off:self.off + w]
        self.off += w
        if dt == BF16:
            a = a.bitcast(BF16)
        a = a[:, 0:n]
        if len(shape) == 2:
            a = a.rearrange("p (a b) -> p a b", a=shape[0])
        elif len(shape) == 3:
            a = a.rearrange("p (a b c) -> p a b c", a=shape[0], b=shape[1])
        return a


class Banks:
    def __init__(self, P, ps):
        self.ps = ps
        self.slots = [Slot(f"bank{i}") for i in range(8)]
        self.ptr = 0

    def get(self, n=1):
        if n > 1:
            self.ptr = (self.ptr + n - 1) // n * n
        if self.ptr + n > 8:
            self.ptr = 0
        b = self.ptr
        self.ptr = (self.ptr + n) % 8
        return b, self.slots[b:b + n]

    def f32(self, b, n=1):
        return self.ps[:, b * 512:(b + n) * 512]

    def bf(self, b, n=1):
        return self.ps[:, b * 512:(b + n) * 512].bitcast(BF16)


def build_program(cfg, debug_outs=(), upto=99):
    D, S, GH, FH, XH, MEM, DFF = (cfg[k] for k in ("D", "S", "GH", "FH", "XH", "MEM", "DFF"))
    KC, NB, GQK, GVW, FW, XW, FC, IN_COLS, T = (cfg[k] for k in ("KC", "NB", "GQK", "GVW", "FW", "XW", "FC", "IN_COLS", "T"))
    NTT = S // T
    TG = T // 512
    BRC = GVW // 128 + FH

    nc = bass.Bass("TRN2", target_bir_lowering=False)

    def din(name, shape, dt=F32):
        return nc.dram_tensor(name, list(shape), dt, kind="ExternalInput").ap()

    def dscr(name, shape, dt):
        kind = "ExternalOutput" if name in debug_outs else "Internal"
        return nc.dram_tensor(name, list(shape), dt, kind=kind).ap()

    x = din("x", [S, D])
    mem = din("mem", [MEM, D])
    w_in = din("w_in", [D, IN_COLS])
    w_gb = din("w_gla_branch", [GVW, D])
    w_fb = din("w_fox_branch", [FW, D])
    w_out = din("w_out", [D, D])
    w_xq = din("w_xa_q", [D, XW])
    w_xkv = din("w_xa_kv", [D, 2 * XW])
    w_xo = din("w_xa_o", [XW, D])
    w_up = din("w_ffn_up", [D, 2 * DFF])
    w_dn = din("w_ffn_down", [DFF, D])
    gcols = din("gcols", [128, 4 * KC])
    gpost = din("gpost", [3, D])
    bgate = din("bgate", [128, 2 * KC])
    wgu = din("wgu", [17, GQK])
    ggla = din("ggla", [1, GVW])
    bfox = din("bfox", [1, FH])
    wconv = din("wconv", [128, 3 * 2 * FC])
    bconv = din("bconv", [128, 2 * FC])
    consts = din("consts", [128, 5 * 128])
    flags = din("flags", [128, 2])
    out = nc.dram_tensor("out", [S, D], F32, kind="ExternalOutput").ap()

    hT = dscr("hT", [KC, 128, S], BF16)
    tm1 = dscr("tm1", [S, 2 * GQK + 2 * GVW], BF16)
    glowT = dscr("glowT", [16, S], F32)
    fqT = dscr("fqT", [FH * 128, S], BF16)
    ffd = dscr("ffd", [S, FH], F32)
    sgT = dscr("sgT", [2 * KC, 128, S], BF16)
    brT = dscr("brT", [BRC, 128, S], BF16)
    mT = dscr("mT", [KC, 128, S], BF16)
    yb = dscr("yb", [S, D], F32)
    memT = dscr("memT", [KC, 128, MEM], BF16)
    kxT = dscr("kxT", [XW // 128, 128, MEM], BF16)
    vx = dscr("vx", [MEM, XW], BF16)
    qxT = dscr("qxT", [XW // 128, 128, S], BF16)
    oxT = dscr("oxT", [XW // 128, 128, S], BF16)
    aT = dscr("aT", [FC, 128, S], BF16)
    gin_fk = nc.dram_tensor("gin_fk", [FH * 128, S], BF16).ap()
    CCB = 2 * 1024 * 1024
    fk_rows = max(128, min(FH * 128, CCB // (S * 2) // 128 * 128))
    gout_fk = [nc.dram_tensor(f"gout_fk{i}", [2 * fk_rows, S], BF16).ap() for i in range(FH * 128 // fk_rows)]
    gin_fv = nc.dram_tensor("gin_fv", [S, FW], BF16).ap()
    fv_rows = max(128, min(S, CCB // (FW * 2) // 128 * 128))
    gout_fv = [nc.dram_tensor(f"gout_fv{i}", [2 * fv_rows, FW], BF16).ap() for i in range(S // fv_rows)]
    gin_rs = nc.dram_tensor("gin_rs", [128, NB * FH], F32).ap()
    gout_rs = nc.dram_tensor("gout_rs", [256, NB * FH], F32).ap()
    gin_st = nc.dram_tensor("gin_st", [128, GH * 256], F32).ap()
    gout_st = nc.dram_tensor("gout_st", [256, GH * 256], F32).ap()
    gin_hl = nc.dram_tensor("gin_hl", [128, KC * 2], BF16).ap()
    gout_hl = nc.dram_tensor("gout_hl", [256, KC * 2], BF16).ap()
    dbg = {}
    if "dbg_st" in debug_outs:
        dbg["st"] = nc.dram_tensor("dbg_st", [128, GH * 256], F32, kind="ExternalOutput").ap()

    ARENA_WORDS = 51200
    es = ExitStack()
    arena_t = es.enter_context(nc.sbuf_tensor("arena", [128, ARENA_WORDS], F32))
    psum_t = es.enter_context(nc.psum_tensor("psum", [128, 8 * 512], F32))
    sems = [es.enter_context(nc.semaphore(f"s{i}")) for i in range(90)]
    P = Prog(nc, sems)
    A = Arena(arena_t[:], ARENA_WORDS)
    BK = Banks(P, psum_t[:])
    PAIRS = [[0, 1], [2, 3], [4, 5], [6, 7]]

    def mm(out, lhsT, rhs, start, stop, reads, writes, signal, skip=False):
        if skip:
            P.op("pe", lambda e, o=out, l=lhsT, r=rhs, s=start, t=stop: e.matmul(o, l, r, start=s, stop=t, skip_group_check=True),
                 reads=reads, writes=writes, signal=signal)
        else:
            P.op("pe", lambda e, o=out, l=lhsT, r=rhs, s=start, t=stop: e.matmul(o, l, r, start=s, stop=t),
                 reads=reads, writes=writes, signal=signal)

    def tr(out, in_, ident, reads, writes, signal):
        P.op("pe", lambda e, o=out, i=in_, d=ident: e.transpose(o, i, d), reads=reads, writes=writes, signal=signal)

    def act(out, in_, func, reads, writes, bias=None, scale=None, accum=None):
        kw = {}
        if bias is not None:
            kw["bias"] = bias
        if scale is not None:
            kw["scale"] = scale
        if accum is not None:
            kw["accum_out"] = accum
        P.op("act", lambda e, o=out, i=in_, f=func, k=kw: e.activation(out=o, in_=i, func=f, **k), reads=reads, writes=writes)

    def tt(eng, out, in0, in1, op, reads, writes):
        P.op(eng, lambda e, o=out, a=in0, b=in1, p=op: e.tensor_tensor(o, a, b, p), reads=reads, writes=writes)

    def ts(eng, out, in0, s1, s2, op0, op1, reads, writes):
        if op1 is None:
            P.op(eng, lambda e, o=out, a=in0, x=s1, p=op0: e.tensor_scalar(o, a, x, None, p), reads=reads, writes=writes)
        else:
            P.op(eng, lambda e, o=out, a=in0, x=s1, y=s2, p=op0, q=op1: e.tensor_scalar(o, a, x, y, p, q), reads=reads, writes=writes)

    def stt(eng, out, in0, sc, in1, op0, op1, reads, writes):
        P.op(eng, lambda e, o=out, a=in0, s=sc, b=in1, p=op0, q=op1: e.scalar_tensor_tensor(o, a, s, b, p, q), reads=reads, writes=writes)

    def cp(eng, out, in_, reads, writes):
        if eng == "act":
            P.op("act", lambda e, o=out, i=in_: e.copy(o, i), reads=reads, writes=writes)
        else:
            P.op(eng, lambda e, o=out, i=in_: e.tensor_copy(o, i), reads=reads, writes=writes)

    def memset(eng, ap, val, writes):
        P.op(eng, lambda e, a=ap, v=val: e.memset(a, v), writes=writes)

    def begin_phase():
        A.reset()

    def end_phase():
        P.barrier()

    def load_consts(which):
        r = {}
        sl = P.slot("consts", dma=True)
        if "ident" in which:
            r["ident"] = A.alloc([128], BF16)
            P.dma("pool", r["ident"], consts[:, 0:128], writes=[sl])
        if "cmask_bf" in which:
            r["cmask_bf"] = A.alloc([128], BF16)
            P.dma("pool", r["cmask_bf"], consts[:, 384:512], writes=[sl])
        if "ones_bf" in which:
            r["ones_bf"] = A.alloc([128], BF16)
            P.dma("pool", r["ones_bf"], consts[:, 512:640], writes=[sl])
        if "f32" in which:
            r["f32"] = A.alloc([5, 128], F32)
            P.dma("sp", r["f32"], consts.rearrange("p (a b) -> p a b", a=5), writes=[sl])
        if "flags" in which:
            r["flags"] = A.alloc([2], F32)
            P.dma("sp", r["flags"], flags, writes=[sl])
        r["slot"] = sl
        return r

    def nr_phase(x_src, n_tok, y_src, gpost_idx, gpre_idx, x_dst, hT_dst, halo=False):
        begin_phase()
        nb = n_tok // 128
        C = load_consts(["ident"])
        gb = gbs = None
        if y_src is not None:
            gb = A.alloc([D], F32)
            gbs = P.slot("gb", dma=True)
            P.dma("sp", gb, gpost[gpost_idx:gpost_idx + 1, :].partition_broadcast(128), writes=[gbs])
        gc = gcs = None
        if hT_dst is not None:
            gc = A.alloc([4 * KC], F32)
            gcs = P.slot("gc", dma=True)
            P.dma("sp", gc, gcols, writes=[gcs])
        xb = [(A.alloc([D], F32), P.slot("x", dma=True)) for _ in range(2)]
        ybuf = [(A.alloc([D], F32), P.slot("y", dma=True)) for _ in range(2)] if y_src is not None else None
        par = [dict(junk=A.alloc([D], BF16), junks=P.slot("junk"), hn=A.alloc([D], BF16), hns=P.slot("hn"),
                    st=A.alloc([8], F32), sts=P.slot("st")) for _ in range(2)]
        tgw = min(512, n_tok)
        hst = [(A.alloc([KC, tgw], BF16), P.slot("hst", dma=True)) for _ in range(1)] if hT_dst is not None else None
        hl = hls = None
        if halo:
            hl = A.alloc([KC, 2], BF16)
            hls = P.slot("hl", dma=True)

        def blk(tb):
            pb = par[tb % 2]
            junk, junks, hn, hns, st, sts = pb["junk"], pb["junks"], pb["hn"], pb["hns"], pb["st"], pb["sts"]
            xt, xs = xb[tb % 2]
            r0 = tb * 128
            P.dma("pool", xt, x_src[r0:r0 + 128, :], writes=[xs])
            if y_src is not None:
                yt, ys = ybuf[tb % 2]
                P.dma("pool", yt, y_src[r0:r0 + 128, :], writes=[ys])
                yield
                act(junk, yt, AF.Square, [ys], [junks, sts], accum=st[:, 0:1])
                yield
                ts("dve", st[:, 1:2], st[:, 0:1], 1.0 / D, EPS, ALU.mult, ALU.add, [sts], [sts])
                yield
                act(st[:, 1:2], st[:, 1:2], AF.Sqrt, [sts], [sts])
                yield
                P.op("dve", lambda e, o=st[:, 1:2], i_=st[:, 1:2]: e.reciprocal(o, i_), reads=[sts], writes=[sts])
                yield
                stt("dve", yt, yt, st[:, 1:2], gb, ALU.mult, ALU.mult, [ys, sts, gbs], [ys])
                yield
                tt("dve", xt, yt, xt, ALU.add, [ys, xs], [xs])
            yield
            if x_dst is not None:
                P.dma("sp", x_dst[r0:r0 + 128, :], xt, reads=[xs])
            if hT_dst is not None:
                act(junk, xt, AF.Square, [xs], [junks, sts], accum=st[:, 2:3])
                yield
                ts("dve", st[:, 3:4], st[:, 2:3], 1.0 / D, EPS, ALU.mult, ALU.add, [sts], [sts])
                yield
                act(st[:, 3:4], st[:, 3:4], AF.Sqrt, [sts], [sts])
                yield
                P.op("dve", lambda e, o=st[:, 3:4], i_=st[:, 3:4]: e.reciprocal(o, i_), reads=[sts], writes=[sts])
                yield
                act(hn, xt, AF.Copy, [xs, sts], [hns], scale=st[:, 3:4])
                yield
                hs_t, hs_s = hst[0]
                tcol = (tb * 128) % tgw
                for k0 in range(0, KC, 8):
                    kn = min(8, KC - k0)
                    b, bs = BK.get(1)
                    pst = BK.bf(b)
                    for k in range(kn):
                        tr(pst[:, k * 128:(k + 1) * 128], hn[:, (k0 + k) * 128:(k0 + k + 1) * 128], C["ident"],
                           [hns, C["slot"]], bs, k == kn - 1)
                    for k in range(kn):
                        kc = k0 + k
                        gsc = gc[:, gpre_idx * KC + kc:gpre_idx * KC + kc + 1]
                        evac(hs_t[:, kc, tcol:tcol + 128], pst[:, k * 128:(k + 1) * 128], bs + [gcs], [hs_s], scale=gsc, disjoint=True)
                    yield
                if halo and tb == nb - 1:
                    cp("dve", hl, hs_t[:, :, tgw - 2:tgw], [hs_s], [hls])
                    P.dma("sp", gin_hl, hl.rearrange("p a b -> p (a b)"), reads=[hls])
                if tcol + 128 == tgw:
                    t0 = tb * 128 + 128 - tgw
                    P.dma("sp", hT_dst.rearrange("k p t -> p k t")[:, :, t0:t0 + tgw], hs_t, reads=[hs_s])

        for tb0 in range(0, nb, 2):
            gens = [blk(tb) for tb in range(tb0, min(tb0 + 2, nb))]
            alive = list(gens)
            while alive:
                for g_ in list(alive):
                    try:
                        next(g_)
                    except StopIteration:
                        alive.remove(g_)
        end_phase()

    def gemm_phase(a_src, akc, n_tok, Tt, wblocks, extra_setup=None, wbuf_kc=None, nwbuf=3):
        begin_phase()
        ctx = extra_setup() if extra_setup is not None else {}
        ntt = n_tok // Tt
        at = A.alloc([akc, Tt], BF16)
        ats = P.slot("A", dma=True)
        wkc = wbuf_kc or akc
        wb = [(A.alloc([wkc, 512], BF16), P.slot(f"w{i}", dma=True)) for i in range(nwbuf)]
        ctx["at"], ctx["ats"] = at, ats
        items = [(tti, blk) for tti in range(ntt) for blk in wblocks]

        def issue(k):
            wt, ws = wb[k % nwbuf]
            for (Wap, k0, nk, c0, ncols, wk0, wc0) in items[k][1]["parts"]:
                Wv = Wap.rearrange("(k p) n -> p k n", p=128)
                for kk in range(0, nk, 8):
                    kn = min(8, nk - kk)
                    P.dma("pool", wt[:, wk0 + kk:wk0 + kk + kn, wc0:wc0 + ncols],
                          Wv[:, k0 + kk:k0 + kk + kn, c0:c0 + ncols], writes=[ws])

        for k in range(min(nwbuf - 1, len(items))):
            issue(k)
        cur_tt = -1
        for k, (tti, blk) in enumerate(items):
            t0 = tti * Tt
            if tti != cur_tt:
                cur_tt = tti
                for k0 in range(0, akc, 8):
                    kn = min(8, akc - k0)
                    P.dma("sp", at[:, k0:k0 + kn, :], a_src.rearrange("k p t -> p k t")[:, k0:k0 + kn, t0:t0 + Tt], writes=[ats])
            if k + nwbuf - 1 < len(items):
                issue(k + nwbuf - 1)
            wt, ws = wb[k % nwbuf]
            for item in blk["tiles"](tti, t0, wt, ws, ctx):
                mmlist, epi = item[0], item[1]
                extra = item[2] if len(item) > 2 else []
                nbk = 1 + max(m[0] for m in mmlist)
                b, bs = BK.get(nbk)
                started = set()
                last_idx = {}
                for i, m in enumerate(mmlist):
                    last_idx[m[0]] = i
                for i, (bi, o_fn, lhsT, rhs) in enumerate(mmlist):
                    mm(o_fn(b + bi), lhsT, rhs, bi not in started, last_idx[bi] == i,
                       [ats, ws] + extra, [bs[bi]], last_idx[bi] == i)
                    started.add(bi)
                epi(b, bs)
        end_phase()

    class Stage:
        def __init__(self, n, shape, dt, name):
            self.bufs = [(A.alloc(shape, dt), P.slot(name, dma=True)) for _ in range(n)]
            self.i = 0

        def next(self):
            r = self.bufs[self.i % len(self.bufs)]
            self.i += 1
            return r

    ev_ctr = [0]

    def evac(out, in_, reads, writes, scale=None, disjoint=False):
        ev_ctr[0] += 1
        if ev_ctr[0] % 2 == 0:
            if scale is None:
                P.op("act", lambda e, o=out, i=in_: e.copy(o, i), reads=reads, writes=writes, disjoint=disjoint)
            else:
                P.op("act", lambda e, o=out, i=in_, s=scale: e.mul(o, i, s), reads=reads, writes=writes, disjoint=disjoint)
        else:
            if scale is None:
                P.op("dve", lambda e, o=out, i=in_: e.tensor_copy(o, i), reads=reads, writes=writes, disjoint=disjoint)
            else:
                P.op("dve", lambda e, o=out, i=in_, x=scale: e.tensor_scalar(o, i, x, None, ALU.mult), reads=reads, writes=writes,
                     disjoint=disjoint)

    def tm_tiles(Tt, nk, ncols, wk0, epi_fn, akc_off=0):
        def f(tti, t0, wt, ws, ctx):
            at = ctx["at"]
            res = []
            for tb in range(Tt // 128):
                ml = [(0, (lambda b, n=ncols: BK.f32(b)[:, 0:n]), at[:, akc_off + k, tb * 128:(tb + 1) * 128],
                       wt[:, wk0 + k, 0:ncols]) for k in range(nk)]
                res.append((ml, (lambda b, bs, r0=t0 + tb * 128: epi_fn(b, bs, r0))))
            return res
        return f

    def fm_tiles(Tt, nk, ncols, wk0, epi_fn, akc_off=0):
        def f(tti, t0, wt, ws, ctx):
            at = ctx["at"]
            res = []
            tgw = min(512, Tt)
            for sub in range((ncols + 127) // 128):
                cw = min(128, ncols - sub * 128)
                for tg in range(Tt // tgw):
                    ml = [(0, (lambda b, c=cw, w=tgw: BK.f32(b)[0:c, 0:w]), wt[:, wk0 + k, sub * 128:sub * 128 + cw],
                           at[:, akc_off + k, tg * tgw:(tg + 1) * tgw]) for k in range(nk)]
                    res.append((ml, (lambda b, bs, s=sub, tt0=t0 + tg * tgw, c=cw, w=tgw: epi_fn(b, bs, s, tt0, c, w))))
            return res
        return f

    if upto > 0:
        nr_phase(x, S, None, 0, 0, None, hT)

    def g1_blocks():
        blocks = []
        stage = {}

        def setup():
            stage["tm"] = Stage(3, [512], BF16, "stm")
            stage["fm"] = Stage(3, [512], BF16, "sfm")
            stage["f32"] = Stage(2, [512], F32, "sf32")
            bg = A.alloc([2 * KC], F32)
            bgs = P.slot("bg", dma=True)
            P.dma("sp", bg, bgate, writes=[bgs])
            stage["bg"], stage["bgs"] = bg, bgs
            return {}

        def tm_store(dst, c0, ncols, dt=BF16):
            def epi(b, bs, r0):
                st_, ss = stage["tm" if dt == BF16 else "f32"].next()
                evac(st_[:, 0:ncols], BK.f32(b)[:, 0:ncols], bs, [ss])
                P.dma("sp", dst[r0:r0 + 128, c0:c0 + ncols], st_[:, 0:ncols], reads=[ss])
            return epi

        def fm_store(dstT_rows, row0, scale=None, dt=BF16):
            def epi(b, bs, sub, tt0, cw, w):
                st_, ss = stage["fm" if dt == BF16 else "f32"].next()
                evac(st_[0:cw, 0:w], BK.f32(b)[0:cw, 0:w], bs, [ss], scale=scale)
                P.dma("sp", dstT_rows[row0 + sub * 128:row0 + sub * 128 + cw, tt0:tt0 + w], st_[0:cw, 0:w], reads=[ss])
            return epi

        def gate_store(cblk):
            def epi(b, bs, sub, tt0, cw, w):
                st_, ss = stage["fm"].next()
                ch = cblk * 4 + sub
                act(st_[:, 0:w], BK.f32(b)[:, 0:w], AF.Sigmoid, bs + [stage["bgs"]], [ss], bias=stage["bg"][:, ch:ch + 1])
                P.dma("sp", sgT[ch, :, tt0:tt0 + w], st_[:, 0:w], reads=[ss])
            return epi

        col = 0
        n_tm1 = 2 * GQK + 2 * GVW
        for c0 in range(0, n_tm1, 512):
            blocks.append(dict(parts=[(w_in, 0, KC, c0, 512, 0, 0)], tiles=tm_tiles(T, KC, 512, 0, tm_store(tm1, c0, 512))))
        col = n_tm1
        blocks.append(dict(parts=[(w_in, 0, KC, col, 16, 0, 0)], tiles=fm_tiles(T, KC, 16, 0, fm_store(glowT, 0, dt=F32))))
        col += 16
        for c0 in range(0, FW, 512):
            blocks.append(dict(parts=[(w_in, 0, KC, col + c0, 512, 0, 0)],
                               tiles=fm_tiles(T, KC, 512, 0, fm_store(fqT, c0, scale=128 ** -0.5))))
        col += FW
        for c0 in range(0, FW, 512):
            blocks.append(dict(parts=[(w_in, 0, KC, col + c0, 512, 0, 0)], tiles=fm_tiles(T, KC, 512, 0, fm_store(gin_fk, c0))))
        col += FW
        for c0 in range(0, FW, 512):
            blocks.append(dict(parts=[(w_in, 0, KC, col + c0, 512, 0, 0)], tiles=tm_tiles(T, KC, 512, 0, tm_store(gin_fv, c0, 512))))
        col += FW
        blocks.append(dict(parts=[(w_in, 0, KC, col, FH, 0, 0)], tiles=tm_tiles(T, KC, FH, 0, tm_store(ffd, 0, FH, dt=F32))))
        col += FH
        for cb in range(2 * D // 512):
            blocks.append(dict(parts=[(w_in, 0, KC, col + cb * 512, 512, 0, 0)], tiles=fm_tiles(T, KC, 512, 0, gate_store(cb))))
        return blocks, setup

    blocks, setup = g1_blocks()
    if upto > 1:
        gemm_phase(hT, KC, S, T, blocks, extra_setup=setup)

    def fox_prep():
        begin_phase()
        C = load_consts(["f32"])
        cf = C["f32"]
        l = A.alloc([NB, FH], F32)
        ls = P.slot("l", dma=True)
        P.dma("sp", l, ffd.rearrange("(n p) h -> p n h", p=128), writes=[ls])
        bf_ = A.alloc([FH], F32)
        bfs = P.slot("bf", dma=True)
        P.dma("sp", bf_, bfox.partition_broadcast(128), writes=[bfs])
        for n in range(NB):
            tt("dve", l[:, n, :], l[:, n, :], bf_, ALU.add, [ls, bfs], [ls])
        l2 = l.rearrange("p n h -> p (n h)")
        act(l2, l2, AF.Exp, [ls], [ls], scale=-1.0)
        act(l2, l2, AF.Ln, [ls], [ls], bias=1.0)
        b, bs = BK.get(1)
        NF = NB * FH
        mm(BK.f32(b)[:, 0:NF], cf[:, 3, :], l2, True, True, [C["slot"], ls], bs, True)
        b2, bs2 = BK.get(1)
        mm(BK.f32(b2)[:, 0:NF], cf[:, 4, :], l2, True, True, [C["slot"], ls], bs2, True)
        tot = A.alloc([NB, FH], F32)
        tots = P.slot("tot")
        cp("dve", tot.rearrange("p n h -> p (n h)"), BK.f32(b2)[:, 0:NF], bs2, [tots])
        off = A.alloc([NB + 1, FH], F32)
        offs = P.slot("off")
        memset("dve", off[:, 0, :], 0.0, [offs])
        for n in range(NB):
            tt("dve", off[:, n + 1, :], off[:, n, :], tot[:, n, :], ALU.add, [offs, tots], [offs])
        lc = A.alloc([NB, FH], F32)
        lcs = P.slot("lc", dma=True)
        tt("dve", lc.rearrange("p n h -> p (n h)"), BK.f32(b)[:, 0:NF], off[:, 0:NB, :].rearrange("p n h -> p (n h)"),
           ALU.add, bs + [offs], [lcs])
        rs = A.alloc([NB, FH], F32)
        rss = P.slot("rs", dma=True)
        for n in range(NB):
            tt("dve", rs[:, n, :], lc[:, n, :], off[:, NB, :], ALU.subtract, [lcs, offs], [rss])
        P.dma("sp", gin_rs, rs.rearrange("p n h -> p (n h)"), reads=[rss])
        end_phase()

    if upto > 2:
        fox_prep()

    def gla_scan(prescan):
        begin_phase()
        C = load_consts(["ident", "cmask_bf", "f32", "flags"])
        cf = C["f32"]
        cs = C["slot"]
        W1 = 2 * GQK + 2 * GVW
        glw = A.alloc([S], F32, parts=32)
        gls = P.slot("glw", dma=True)
        memset("dve", glw, 1.0, [gls])
        P.dma("sp", glw[0:16, :], glowT, writes=[gls])
        wg = A.alloc([GQK], F32, parts=32)
        wgs = P.slot("wg", dma=True)
        P.dma("sp", wg[0:17, :], wgu, writes=[wgs])
        Sst = A.alloc([GH, 256], F32)
        Sbf = A.alloc([GH, 256], BF16)
        Ss = [P.slot(f"S{h}") for h in range(GH)]
        Sbs = [P.slot(f"Sb{h}") for h in range(GH)]
        Sld = P.slot("Sld", dma=True)
        if prescan:
            memset("dve", Sst.rearrange("p a b -> p (a b)"), 0.0, Ss)
        else:
            P.dma("sp", Sst.rearrange("p a b -> p (a b)"), gout_st[0:128, :], writes=[Sld])
            for h in range(GH):
                ts("dve", Sst[:, h, :], Sst[:, h, :], C["flags"][:, 0:1], None, ALU.mult, None, [Sld, cs], [Ss[h]])
                cp("act", Sbf[:, h, :], Sst[:, h, :], [Ss[h]], [Sbs[h]])
            gg = A.alloc([GVW], F32)
            ggs = P.slot("gg", dma=True)
            P.dma("sp", gg, ggla.partition_broadcast(128), writes=[ggs])
        tmb = [(A.alloc([W1], BF16), P.slot("tm", dma=True)) for _ in range(2)]
        e_ = A.alloc([GQK], F32)
        es_ = P.slot("e")
        l_ = A.alloc([GQK], F32)
        ls_ = P.slot("l")
        ed = A.alloc([GQK], F32)
        eds = P.slot("ed")
        kd = A.alloc([GQK], BF16)
        kds = P.slot("kd")
        dec = A.alloc([GH], F32)
        decs = P.slot("dec")
        if not prescan:
            eb = A.alloc([GQK], F32)
            ebs = P.slot("eb")
            enb = A.alloc([GQK], F32)
            enbs = P.slot("enb")
            qt_ = A.alloc([GQK], BF16)
            qts = P.slot("qt")
            kt_ = A.alloc([GQK], BF16)
            kts = P.slot("kt")
            qT = A.alloc([GH, 128], BF16)
            qTs = P.slot("qT")
            kT = A.alloc([GH, 128], BF16)
            kTs = P.slot("kT")
            att = A.alloc([GH, 128], BF16)
            atts = P.slot("att")
            G = A.alloc([GVW], F32)
            Gs = P.slot("G")
            og = A.alloc([GVW], BF16)
            ogs = P.slot("og")
            st = A.alloc([2 * GH], F32)
            sts = P.slot("st")
            junk = A.alloc([256], BF16)
            junks = P.slot("junk")
            tgw = min(512, S)
            ogst = [(A.alloc([GVW // 128, tgw], BF16), P.slot("ogst", dma=True)) for _ in range(2)]
        for tb in range(NB):
            r0 = tb * 128
            tmt, tms = tmb[tb % 2]
            P.dma("pool", tmt, tm1[r0:r0 + 128, :], writes=[tms])
            qv = tmt[:, 0:GQK]
            kv = tmt[:, GQK:2 * GQK]
            vv = tmt[:, 2 * GQK:2 * GQK + GVW]
            rv = tmt[:, 2 * GQK + GVW:W1]
            nh = (GQK + 511) // 512
            bz, bzs = BK.get(nh)
            for i in range(nh):
                cw = min(512, GQK - i * 512)
                mm(BK.f32(bz + i)[:, 0:cw], glw[0:17, r0:r0 + 128], wg[0:17, i * 512:i * 512 + cw], True, True, [gls, wgs], [bzs[i]], True)
            zps = BK.f32(bz, nh)[:, 0:GQK] if GQK % 512 == 0 else None
            for i in range(nh):
                cw = min(512, GQK - i * 512)
                act(e_[:, i * 512:i * 512 + cw], BK.f32(bz + i)[:, 0:cw], AF.Exp, [bzs[i]], [es_], scale=-1.0)
            act(l_, e_, AF.Ln, [es_], [ls_], bias=1.0)
            bd, bds = BK.get(nh)
            for i in range(nh):
                cw = min(512, GQK - i * 512)
                mm(BK.f32(bd + i)[:, 0:cw], cf[:, 2, :], l_[:, i * 512:i * 512 + cw], True, True, [cs, ls_], [bds[i]], True)
            for i in range(nh):
                cw = min(512, GQK - i * 512)
                act(ed[:, i * 512:i * 512 + cw], BK.f32(bd + i)[:, 0:cw], AF.Exp, [bds[i]], [eds])
            bt_, bts = BK.get(1)
            for h in range(GH):
                mm(BK.f32(bt_)[:, h:h + 1], l_[:, h * 128:(h + 1) * 128], cf[:, 1, 127:128], True, True, [ls_, cs], bts, h == GH - 1)
            act(dec, BK.f32(bt_)[:, 0:GH], AF.Exp, bts, [decs])
            tt("pool", kd, kv, ed, ALU.mult, [tms, eds], [kds])
            if not prescan:
                bb, bbs = BK.get(nh)
                for i in range(nh):
                    cw = min(512, GQK - i * 512)
                    mm(BK.f32(bb + i)[:, 0:cw], cf[:, 1, :], l_[:, i * 512:i * 512 + cw], True, True, [cs, ls_], [bbs[i]], True)
                for i in range(nh):
                    cw = min(512, GQK - i * 512)
                    act(eb[:, i * 512:i * 512 + cw], BK.f32(bb + i)[:, 0:cw], AF.Exp, [bbs[i]], [ebs])
                    act(enb[:, i * 512:i * 512 + cw], BK.f32(bb + i)[:, 0:cw], AF.Exp, [bbs[i]], [enbs], scale=-1.0)
                stt("dve", qt_, qv, 128 ** -0.5, eb, ALU.mult, ALU.mult, [tms, ebs], [qts])
                tt("dve", kt_, kv, enb, ALU.mult, [tms, enbs], [kts])
                hg = min(GH, 8)
                bq, bqs = BK.get(1)
                for h in range(GH):
                    tr(BK.bf(bq)[:, h * 128:(h + 1) * 128], qt_[:, h * 128:(h + 1) * 128], C["ident"], [qts, cs], bqs, h == GH - 1)
                cp("act", qT.rearrange("p a b -> p (a b)"), BK.bf(bq)[:, 0:GQK], bqs, [qTs])
                bk_, bks = BK.get(1)
                for h in range(GH):
                    tr(BK.bf(bk_)[:, h * 128:(h + 1) * 128], kt_[:, h * 128:(h + 1) * 128], C["ident"], [kts, cs], bks, h == GH - 1)
                cp("dve", kT.rearrange("p a b -> p (a b)"), BK.bf(bk_)[:, 0:GQK], bks, [kTs])
                for h0 in range(0, GH, 4):
                    hn_ = min(4, GH - h0)
                    ba, bas = BK.get(1)
                    for h in range(h0, h0 + hn_):
                        mm(BK.f32(ba)[:, (h - h0) * 128:(h - h0 + 1) * 128], kT[:, h, :], qT[:, h, :], True, True,
                           [kTs, qTs], bas, h == h0 + hn_ - 1)
                    for h in range(h0, h0 + hn_):
                        tt("dve", att[:, h, :], BK.f32(ba)[:, (h - h0) * 128:(h - h0 + 1) * 128], C["cmask_bf"], ALU.mult,
                           bas + [cs], [atts])
                act(G, rv, AF.Silu, [tms], [Gs])
                tt("pool", G, G, gg, ALU.mult, [Gs, ggs], [Gs])
                for h0 in range(0, GH, 2):
                    bo, bos = BK.get(1)
                    for h in range(h0, min(h0 + 2, GH)):
                        o_ps = BK.f32(bo)[:, (h - h0) * 256:(h - h0 + 1) * 256]
                        mm(o_ps, att[:, h, :], vv[:, h * 256:(h + 1) * 256], True, False, [atts, tms], bos, False)
                        mm(o_ps, qT[:, h, :], Sbf[:, h, :], False, True, [qTs, Sbs[h]], bos, True)
                    for h in range(h0, min(h0 + 2, GH)):
                        o_ps = BK.f32(bo)[:, (h - h0) * 256:(h - h0 + 1) * 256]
                        act(junk, o_ps, AF.Square, bos, [junks, sts], accum=st[:, h:h + 1])
                        ts("dve", st[:, GH + h:GH + h + 1], st[:, h:h + 1], 1.0 / 256, EPS, ALU.mult, ALU.add, [sts], [sts])
                        act(st[:, GH + h:GH + h + 1], st[:, GH + h:GH + h + 1], AF.Sqrt, [sts], [sts])
                        P.op("dve", lambda e, o=st[:, GH + h:GH + h + 1], i_=st[:, GH + h:GH + h + 1]: e.reciprocal(o, i_), reads=[sts], writes=[sts])
                        stt("dve", og[:, h * 256:(h + 1) * 256], o_ps, st[:, GH + h:GH + h + 1], G[:, h * 256:(h + 1) * 256],
                            ALU.mult, ALU.mult, bos + [sts, Gs], [ogs])
                ost, oss = ogst[(tb * 128 // tgw) % 2]
                tcol = (tb * 128) % tgw
                nchk = GVW // 128
                for c0 in range(0, nchk, 8):
                    cn = min(8, nchk - c0)
                    bt2, bt2s = BK.get(1)
                    for c in range(cn):
                        tr(BK.bf(bt2)[:, c * 128:(c + 1) * 128], og[:, (c0 + c) * 128:(c0 + c + 1) * 128], C["ident"], [ogs, cs], bt2s, c == cn - 1)
                    for c in range(cn):
                        evac(ost[:, c0 + c, tcol:tcol + 128], BK.bf(bt2)[:, c * 128:(c + 1) * 128], bt2s, [oss], disjoint=True)
                if tcol + 128 == tgw:
                    t0 = tb * 128 + 128 - tgw
                    P.dma("sp", brT[0:nchk].rearrange("k p t -> p k t")[:, :, t0:t0 + tgw], ost, reads=[oss])
            for h0 in range(0, GH, 2):
                bu, bus = BK.get(1)
                for h in range(h0, min(h0 + 2, GH)):
                    mm(BK.f32(bu)[:, (h - h0) * 256:(h - h0 + 1) * 256], kd[:, h * 128:(h + 1) * 128], vv[:, h * 256:(h + 1) * 256],
                       True, True, [kds, tms], bus, h == min(h0 + 2, GH) - 1)
                for h in range(h0, min(h0 + 2, GH)):
                    stt("dve", Sst[:, h, :], Sst[:, h, :], dec[:, h:h + 1], BK.f32(bu)[:, (h - h0) * 256:(h - h0 + 1) * 256],
                        ALU.mult, ALU.add, [Ss[h], decs] + bus, [Ss[h]])
                    if not prescan:
                        cp("act", Sbf[:, h, :], Sst[:, h, :], [Ss[h]], [Sbs[h]])
        if prescan:
            P.dma("sp", gin_st, Sst.rearrange("p a b -> p (a b)"), reads=Ss, sem=Sld.sem)
            if "st" in dbg:
                P.dma("sp", dbg["st"], Sst.rearrange("p a b -> p (a b)"), reads=Ss, sem=Sld.sem)
        end_phase()

    if upto > 3:
        gla_scan(True)

    def collectives(pairs):
        begin_phase()
        X = P.E["pool"]
        for (i_, o_) in pairs:
            s = P.new_sem()
            s.count += 1
            X.items.append(("o", lambda e, a=i_, b=o_: e.collective_compute("AllGather", ALU.bypass, replica_groups=PAIRS,
                                                                            ins=[a.opt()], outs=[b.opt()]), s, 1))
            X.idx += 1
        end_phase()

    if upto > 4:
        collectives([(gin_fk[i * fk_rows:(i + 1) * fk_rows, :], gout_fk[i]) for i in range(len(gout_fk))]
                    + [(gin_fv[i * fv_rows:(i + 1) * fv_rows, :], gout_fv[i]) for i in range(len(gout_fv))]
                    + [(gin_rs, gout_rs), (gin_st, gout_st)])

    if upto > 5:
        gla_scan(False)

    def fox_attn():
        begin_phase()
        C = load_consts(["cmask_bf", "ones_bf", "f32", "flags"])
        cs = C["slot"]
        cf = C["f32"]
        l = A.alloc([NB, FH], F32)
        ls = P.slot("l", dma=True)
        P.dma("sp", l, ffd.rearrange("(n p) h -> p n h", p=128), writes=[ls])
        bf_ = A.alloc([FH], F32)
        bfs = P.slot("bf", dma=True)
        P.dma("sp", bf_, bfox.partition_broadcast(128), writes=[bfs])
        for n in range(NB):
            tt("dve", l[:, n, :], l[:, n, :], bf_, ALU.add, [ls, bfs], [ls])
        l2 = l.rearrange("p n h -> p (n h)")
        act(l2, l2, AF.Exp, [ls], [ls], scale=-1.0)
        act(l2, l2, AF.Ln, [ls], [ls], bias=1.0)
        NF = NB * FH
        b, bs = BK.get(1)
        mm(BK.f32(b)[:, 0:NF], cf[:, 3, :], l2, True, True, [cs, ls], bs, True)
        b2, bs2 = BK.get(1)
        mm(BK.f32(b2)[:, 0:NF], cf[:, 4, :], l2, True, True, [cs, ls], bs2, True)
        tot = A.alloc([NB, FH], F32)
        tots = P.slot("tot")
        cp("dve", tot.rearrange("p n h -> p (n h)"), BK.f32(b2)[:, 0:NF], bs2, [tots])
        off = A.alloc([NB + 1, FH], F32)
        offs = P.slot("off")
        memset("dve", off[:, 0, :], 0.0, [offs])
        for n in range(NB):
            tt("dve", off[:, n + 1, :], off[:, n, :], tot[:, n, :], ALU.add, [offs, tots], [offs])
        kb = A.alloc([2 * NB, FH], F32)
        kbs = P.slot("kb", dma=True)
        P.dma("sp", kb[:, 0:NB, :].rearrange("p n h -> p (n h)"), gout_rs[0:128, :], writes=[kbs])
        ts("dve", kb[:, 0:NB, :].rearrange("p n h -> p (n h)"), kb[:, 0:NB, :].rearrange("p n h -> p (n h)"),
           C["flags"][:, 1:2], None, ALU.add, None, [kbs, cs], [kbs])
        tt("dve", kb[:, NB:2 * NB, :].rearrange("p n h -> p (n h)"), BK.f32(b)[:, 0:NF],
           off[:, 0:NB, :].rearrange("p n h -> p (n h)"), ALU.add, bs + [offs], [kbs])
        Kb = [(A.alloc([2 * S], BF16), P.slot("K", dma=True)) for _ in range(2)]
        Vb = [(A.alloc([2 * NB, 128], BF16), P.slot("V", dma=True)) for _ in range(2)]
        Qb = [(A.alloc([S], BF16), P.slot("Q", dma=True)) for _ in range(2)]
        Ob = [(A.alloc([S], BF16), P.slot("O", dma=True)) for _ in range(2)]
        pTb = [(A.alloc([512], BF16), P.slot("pT")) for _ in range(4)]
        crb = [(A.alloc([S], BF16, parts=32), P.slot("crow")) for _ in range(2)]
        rinv = A.alloc([512], F32)
        rinvs = P.slot("rinv")
        NG4 = NB // 4
        ctr = dict(p=0, q=0, s=0)
        steps = []

        def head_loads(h):
            Kt, Ks = Kb[h % 2]
            Vt, Vs = Vb[h % 2]
            Qt, Qs = Qb[h % 2]
            kr = (h * 128) % fk_rows
            P.dma("sp", Kt[:, 0:S], gout_fk[(h * 128) // fk_rows][kr:kr + 128, :], writes=[Ks])
            P.dma("sp", Kt[:, S:2 * S], gin_fk[h * 128:(h + 1) * 128, :], writes=[Ks])
            for vi in range(len(gout_fv)):
                nbv = fv_rows // 128
                P.dma("sp", Vt[:, vi * nbv:(vi + 1) * nbv, :],
                      gout_fv[vi][0:fv_rows, h * 128:(h + 1) * 128].rearrange("(n p) d -> p n d", p=128), writes=[Vs])
            P.dma("sp", Vt[:, NB:2 * NB, :], gin_fv[:, h * 128:(h + 1) * 128].rearrange("(n p) d -> p n d", p=128), writes=[Vs])
            P.dma("sp", Qt, fqT[h * 128:(h + 1) * 128, :], writes=[Qs])
            cr, crs = crb[h % 2]
            for i in range(NB):
                ts("dve", cr[0:1, i * 128:(i + 1) * 128], cf[0:1, 4, :], off[0:1, i + 1, h:h + 1], -1.0, ALU.mult, ALU.mult,
                   [cs, offs], [crs])

        for h in range(FH):
            for g in range(NG4):
                nwide = NB + 4 * g
                nsteps = nwide + 4
                for si in range(nsteps):
                    st_ = {}

                    def S_fn(h=h, g=g, si=si, nwide=nwide, st_=st_):
                        if g == 0 and si == 0:
                            head_loads(h)
                        Kt, Ks = Kb[h % 2]
                        Qt, Qs = Qb[h % 2]
                        cr, crs = crb[h % 2]
                        bsx = 4 + ctr["s"] % 4
                        ctr["s"] += 1
                        bsxs = [BK.slots[bsx]]
                        st_["bsx"], st_["bsxs"] = bsx, bsxs
                        if si < nwide:
                            j = si
                            mm(BK.f32(bsx), Kt[:, j * 128:(j + 1) * 128], Qt[:, g * 512:(g + 1) * 512], True, False, [Ks, Qs], bsxs, False)
                            mm(BK.f32(bsx), C["ones_bf"][0:1, :], cr[0:1, g * 512:(g + 1) * 512], False, True, [cs, crs], bsxs, True)
                        else:
                            ii = si - nwide
                            i = 4 * g + ii
                            for jj in range(ii + 1):
                                j = nwide + jj
                                o_ = BK.f32(bsx)[:, jj * 128:(jj + 1) * 128]
                                mm(o_, Kt[:, j * 128:(j + 1) * 128], Qt[:, i * 128:(i + 1) * 128], True, False, [Ks, Qs], bsxs, False)
                                mm(o_, C["ones_bf"][0:1, :], cr[0:1, i * 128:(i + 1) * 128], False, True, [cs, crs], bsxs, jj == ii)

                    def EXP_fn(h=h, g=g, si=si, nwide=nwide, st_=st_):
                        bsx, bsxs = st_["bsx"], st_["bsxs"]
                        pT, pTs = pTb[ctr["p"] % 4]
                        ctr["p"] += 1
                        st_["pT"], st_["pTs"] = pT, pTs
                        if si < nwide:
                            act(pT, BK.f32(bsx), AF.Exp, bsxs + [kbs], [pTs], bias=kb[:, si, h:h + 1])
                        else:
                            ii = si - nwide
                            for jj in range(ii + 1):
                                j = nwide + jj
                                act(pT[:, jj * 128:(jj + 1) * 128], BK.f32(bsx)[:, jj * 128:(jj + 1) * 128], AF.Exp, bsxs + [kbs], [pTs],
                                    bias=kb[:, j, h:h + 1])
                            tt("dve", pT[:, ii * 128:(ii + 1) * 128], pT[:, ii * 128:(ii + 1) * 128], C["cmask_bf"], ALU.mult, [pTs, cs], [pTs])

                    def PV_fn(h=h, g=g, si=si, nwide=nwide, nsteps=nsteps, st_=st_):
                        Vt, Vs = Vb[h % 2]
                        Ot, Os = Ob[h % 2]
                        pT, pTs = st_["pT"], st_["pTs"]
                        if si == 0:
                            ctr["q"] += 1
                        bo = 2 * (ctr["q"] % 2)
                        br = bo + 1
                        bos, brs = [BK.slots[bo]], [BK.slots[br]]
                        o_ps = BK.f32(bo)
                        r_ps = BK.f32(br)
                        if si < nwide:
                            j = si
                            mm(o_ps, Vt[:, j, :], pT, j == 0, False, [Vs, pTs], bos, False)
                            mm(r_ps, C["ones_bf"], pT, j == 0, False, [cs, pTs], brs, False)
                        else:
                            ii = si - nwide
                            for jj in range(ii + 1):
                                j = nwide + jj
                                lastq = (jj == ii)
                                mm(o_ps[:, ii * 128:(ii + 1) * 128], Vt[:, j, :], pT[:, jj * 128:(jj + 1) * 128], False, lastq, [Vs, pTs], bos,
                                   lastq and ii == 3, skip=True)
                                mm(r_ps[:, ii * 128:(ii + 1) * 128], C["ones_bf"], pT[:, jj * 128:(jj + 1) * 128], False, lastq, [cs, pTs], brs,
                                   lastq and ii == 3, skip=True)
                        if si == nsteps - 1:
                            P.op("dve", lambda e, o=rinv, i_=r_ps: e.reciprocal(o, i_), reads=brs, writes=[rinvs])
                            tt("dve", Ot[:, g * 512:(g + 1) * 512], o_ps, rinv, ALU.mult, bos + [rinvs], [Os])
                            if g == NG4 - 1:
                                P.dma("sp", brT[GVW // 128 + h], Ot, reads=[Os])
                    steps.append((S_fn, EXP_fn, PV_fn))
        for k in range(len(steps) + 1):
            if k < len(steps):
                steps[k][0]()
                steps[k][1]()
            if k >= 1:
                steps[k - 1][2]()
        end_phase()

    if upto > 6:
        fox_attn()

    def g2():
        NG = GVW // 128
        st = {}

        def setup():
            st["g"] = [(A.alloc([2, 512], BF16), P.slot("g", dma=True)) for _ in range(2)]
            st["tmp"] = A.alloc([512], F32)
            st["tmps"] = P.slot("tmp")
            st["out"] = Stage(2, [512], BF16, "mo")
            st["i"] = 0
            return {}

        def tiles_for(cblk):
            def f(tti, t0, wt, ws, ctx):
                at = ctx["at"]
                res = []
                tgw = min(512, T)
                for sub in range(4):
                    ch = cblk * 4 + sub
                    for tg in range(T // tgw):
                        ml = [(0, (lambda b, w=tgw: BK.f32(b)[:, 0:w]), wt[:, k, sub * 128:(sub + 1) * 128], at[:, k, tg * tgw:(tg + 1) * tgw]) for k in range(NG)]
                        ml += [(1, (lambda b, w=tgw: BK.f32(b)[:, 0:w]), wt[:, NG + k, sub * 128:(sub + 1) * 128], at[:, NG + k, tg * tgw:(tg + 1) * tgw]) for k in range(FH)]

                        def epi(b, bs, ch=ch, tt0=t0 + tg * tgw, w=tgw):
                            gt, gs = st["g"][st["i"] % 2]
                            st["i"] += 1
                            P.dma("sp", gt[:, 0, 0:w], sgT[ch, :, tt0:tt0 + w], writes=[gs])
                            P.dma("sp", gt[:, 1, 0:w], sgT[KC + ch, :, tt0:tt0 + w], writes=[gs])
                            so, sos = st["out"].next()
                            tt("dve", st["tmp"][:, 0:w], BK.f32(b)[:, 0:w], gt[:, 0, 0:w], ALU.mult, [bs[0], gs], [st["tmps"]])
                            tt("dve", so[:, 0:w], BK.f32(b + 1)[:, 0:w], gt[:, 1, 0:w], ALU.mult, [bs[1], gs], [sos])
                            tt("dve", so[:, 0:w], so[:, 0:w], st["tmp"][:, 0:w], ALU.add, [sos, st["tmps"]], [sos])
                            P.dma("sp", mT[ch, :, tt0:tt0 + w], so[:, 0:w], reads=[sos])
                        res.append((ml, epi))
                return res
            return f

        blocks = []
        for cb in range(D // 512):
            blocks.append(dict(parts=[(w_gb, 0, NG, cb * 512, 512, 0, 0), (w_fb, 0, FH, cb * 512, 512, NG, 0)], tiles=tiles_for(cb)))
        gemm_phase(brT, BRC, S, T, blocks, extra_setup=setup)

    if upto > 7:
        g2()

    def tm_gemm_to_y(a_src, akc, W, Tt, kparts=1):
        st = {}

        def setup():
            st["s"] = Stage(3, [512], F32, "ys")
            return {}

        def y_store(c0):
            def epi(b, bs, r0):
                s_, ss = st["s"].next()
                evac(s_, BK.f32(b), bs, [ss])
                P.dma("sp", yb[r0:r0 + 128, c0:c0 + 512], s_, reads=[ss])
            return epi
        blocks = []
        if kparts == 1:
            for c0 in range(0, D, 512):
                blocks.append(dict(parts=[(W, 0, akc, c0, 512, 0, 0)], tiles=tm_tiles(Tt, akc, 512, 0, y_store(c0))))
            gemm_phase(a_src, akc, S, Tt, blocks, extra_setup=setup)
        else:
            kh = (akc + 1) // 2
            ntb = Tt // 128
            begin_phase()
            setup()
            at = A.alloc([akc, Tt], BF16)
            ats = P.slot("A", dma=True)
            wb = [(A.alloc([kh, 512], BF16), P.slot(f"w{i}", dma=True)) for i in range(2)]
            items = [(tti, c0, half) for tti in range(S // Tt) for c0 in range(0, D, 512) for half in range(2)]
            Wv = W.rearrange("(k p) n -> p k n", p=128)

            def issue(k):
                wt, ws = wb[k % 2]
                _, c0, half = items[k]
                k0, nk = (0, kh) if half == 0 else (kh, akc - kh)
                for kk in range(0, nk, 8):
                    kn = min(8, nk - kk)
                    P.dma("pool", wt[:, kk:kk + kn, :], Wv[:, k0 + kk:k0 + kk + kn, c0:c0 + 512], writes=[ws])

            issue(0)
            cur_tt = -1
            bb = bbs = None
            for k, (tti, c0, half) in enumerate(items):
                t0 = tti * Tt
                if tti != cur_tt:
                    cur_tt = tti
                    for k0 in range(0, akc, 8):
                        kn = min(8, akc - k0)
                        P.dma("sp", at[:, k0:k0 + kn, :], a_src.rearrange("k p t -> p k t")[:, k0:k0 + kn, t0:t0 + Tt], writes=[ats])
                if k + 1 < len(items):
                    issue(k + 1)
                wt, ws = wb[k % 2]
                if half == 0:
                    bb, bbs = BK.get(ntb)
                k0, nk = (0, kh) if half == 0 else (kh, akc - kh)
                for tb in range(ntb):
                    for kq in range(nk):
                        last = (half == 1 and kq == nk - 1)
                        mm(BK.f32(bb + tb), at[:, k0 + kq, tb * 128:(tb + 1) * 128], wt[:, kq, :], half == 0 and kq == 0, last,
                           [ats, ws], [bbs[tb]], last or kq == nk - 1)
                if half == 1:
                    for tb in range(ntb):
                        y_store(c0)(bb + tb, [bbs[tb]], t0 + tb * 128)
            end_phase()

    if upto > 8:
        tm_gemm_to_y(mT, KC, w_out, T)

    if upto > 9:
        nr_phase(x, S, yb, 0, 1, out, hT)

    if upto > 10:
        nr_phase(mem, MEM, None, 0, 2, None, memT)

    def xa_kv():
        st = {}

        def setup():
            st["fm"] = Stage(2, [512], BF16, "sfm")
            st["tm"] = Stage(2, [512], BF16, "stm")
            return {}

        def k_store(c0):
            def epi(b, bs, sub, tt0, cw, w):
                s_, ss = st["fm"].next()
                evac(s_[:, 0:w], BK.f32(b)[:, 0:w], bs, [ss])
                P.dma("sp", kxT[c0 // 128 + sub, :, tt0:tt0 + w], s_[:, 0:w], reads=[ss])
            return epi

        def v_store(c0):
            def epi(b, bs, r0):
                s_, ss = st["tm"].next()
                evac(s_, BK.f32(b), bs, [ss])
                P.dma("sp", vx[r0:r0 + 128, c0:c0 + 512], s_, reads=[ss])
            return epi
        blocks = []
        for c0 in range(0, XW, 512):
            blocks.append(dict(parts=[(w_xkv, 0, KC, c0, 512, 0, 0)], tiles=fm_tiles(MEM, KC, 512, 0, k_store(c0))))
        for c0 in range(0, XW, 512):
            blocks.append(dict(parts=[(w_xkv, 0, KC, XW + c0, 512, 0, 0)], tiles=tm_tiles(MEM, KC, 512, 0, v_store(c0))))
        gemm_phase(memT, KC, MEM, MEM, blocks, extra_setup=setup)

    if upto > 11:
        xa_kv()

    def xa_q():
        st = {}

        def setup():
            st["fm"] = Stage(3, [512], BF16, "sfm")
            return {}

        def q_store(c0):
            def epi(b, bs, sub, tt0, cw, w):
                s_, ss = st["fm"].next()
                evac(s_[:, 0:w], BK.f32(b)[:, 0:w], bs, [ss], scale=256 ** -0.5)
                P.dma("sp", qxT[c0 // 128 + sub, :, tt0:tt0 + w], s_[:, 0:w], reads=[ss])
            return epi
        blocks = []
        for c0 in range(0, XW, 512):
            cw = min(512, XW - c0)
            blocks.append(dict(parts=[(w_xq, 0, KC, c0, cw, 0, 0)], tiles=fm_tiles(T, KC, cw, 0, q_store(c0))))
        gemm_phase(hT, KC, S, T, blocks, extra_setup=setup)

    if upto > 12:
        xa_q()

    def xa_attn():
        begin_phase()
        C = load_consts(["ones_bf"])
        cs = C["slot"]
        XC = XW // 128
        MB = MEM // 128
        kt = A.alloc([XC, MEM], BF16)
        kts = P.slot("kx", dma=True)
        P.dma("sp", kt, kxT.rearrange("k p t -> p k t"), writes=[kts])
        vt = A.alloc([MB, XW], BF16)
        vts = P.slot("vx", dma=True)
        P.dma("sp", vt, vx.rearrange("(n p) d -> p n d", p=128), writes=[vts])
        tgw = min(512, S)
        qb = [(A.alloc([XC, tgw], BF16), P.slot("q", dma=True)) for _ in range(2)]
        ob = [(A.alloc([XC, tgw], BF16), P.slot("o", dma=True)) for _ in range(2)]
        pb = [(A.alloc([tgw], BF16), P.slot("p")) for _ in range(3)]
        rinv = A.alloc([tgw], F32)
        rinvs = P.slot("rinv")
        pi = 0
        for tg in range(S // tgw):
            t0 = tg * tgw
            qt, qs = qb[tg % 2]
            ot, os_ = ob[tg % 2]
            P.dma("sp", qt, qxT.rearrange("k p t -> p k t")[:, :, t0:t0 + tgw], writes=[qs])
            for h in range(XH):
                bo0, bo0s = BK.get(1)
                bo1, bo1s = BK.get(1)
                br, brs = BK.get(1)
                for m in range(MB):
                    bsx, bsxs = BK.get(1)
                    for c in range(2):
                        mm(BK.f32(bsx)[:, 0:tgw], kt[:, 2 * h + c, m * 128:(m + 1) * 128], qt[:, 2 * h + c, :], c == 0, c == 1,
                           [kts, qs], bsxs, c == 1)
                    pt, ps_ = pb[pi % 3]
                    pi += 1
                    act(pt, BK.f32(bsx)[:, 0:tgw], AF.Exp, bsxs, [ps_])
                    mm(BK.f32(bo0)[:, 0:tgw], vt[:, m, h * 256:h * 256 + 128], pt, m == 0, m == MB - 1, [vts, ps_], bo0s, m == MB - 1)
                    mm(BK.f32(bo1)[:, 0:tgw], vt[:, m, h * 256 + 128:h * 256 + 256], pt, m == 0, m == MB - 1, [vts, ps_], bo1s, m == MB - 1)
                    mm(BK.f32(br)[:, 0:tgw], C["ones_bf"], pt, m == 0, m == MB - 1, [cs, ps_], brs, m == MB - 1)
                P.op("dve", lambda e, o=rinv, i_=BK.f32(br)[:, 0:tgw]: e.reciprocal(o, i_), reads=brs, writes=[rinvs])
                tt("dve", ot[:, 2 * h, :], BK.f32(bo0)[:, 0:tgw], rinv, ALU.mult, bo0s + [rinvs], [os_])
                tt("dve", ot[:, 2 * h + 1, :], BK.f32(bo1)[:, 0:tgw], rinv, ALU.mult, bo1s + [rinvs], [os_])
            P.dma("sp", oxT.rearrange("k p t -> p k t")[:, :, t0:t0 + tgw], ot, reads=[os_])
        end_phase()

    if upto > 13:
        xa_attn()
    if upto > 14:
        tm_gemm_to_y(oxT, XW // 128, w_xo, T)
    if upto > 15:
        nr_phase(out, S, yb, 1, 3, out, hT, halo=True)
    if upto > 16:
        collectives([(gin_hl, gout_hl)])

    def g6():
        st = {}

        def setup():
            C = load_consts(["flags"])
            st["C"] = C
            st["wc"] = A.alloc([3, 2 * FC], F32)
            st["wcs"] = P.slot("wc", dma=True)
            P.dma("sp", st["wc"], wconv.rearrange("p (a b) -> p a b", a=3), writes=[st["wcs"]])
            st["bc"] = A.alloc([2 * FC], F32)
            P.dma("sp", st["bc"], bconv, writes=[st["wcs"]])
            st["hstore"] = A.alloc([2 * FC, 2], F32)
            st["hss"] = P.slot("hstore")
            st["hh"] = A.alloc([KC, 2], BF16)
            st["hhs"] = P.slot("hh", dma=True)
            P.dma("sp", st["hh"].rearrange("p a b -> p (a b)"), gout_hl[0:128, :], writes=[st["hhs"]])
            st["ue"] = [(A.alloc([2, 520], F32), P.slot("ue")) for _ in range(2)]
            st["cg"] = A.alloc([512], F32)
            st["cgs"] = P.slot("cg")
            st["cu"] = A.alloc([512], F32)
            st["cus"] = P.slot("cu")
            st["t1"] = A.alloc([512], F32)
            st["t1s"] = P.slot("t1")
            st["sg"] = A.alloc([512], F32)
            st["sgs"] = P.slot("sg")
            st["out"] = Stage(2, [512], BF16, "ao")
            st["i"] = 0
            return {}

        def tiles_for(c0, cw):
            def f(tti, t0, wt, ws, ctx):
                at = ctx["at"]
                res = []
                tgw = min(512, T)
                for sub in range(cw // 128):
                    ci = c0 // 128 + sub
                    if tti == 0:
                        ml = [(0, (lambda b: BK.f32(b)[:, 0:2]), wt[:, k, sub * 128:(sub + 1) * 128], st["hh"][:, k, :]) for k in range(KC)]
                        ml += [(1, (lambda b: BK.f32(b)[:, 0:2]), wt[:, k, 256 + sub * 128:256 + (sub + 1) * 128], st["hh"][:, k, :]) for k in range(KC)]

                        def epi_h(b, bs, ci=ci):
                            fl = st["C"]["flags"][:, 0:1]
                            ts("dve", st["hstore"][:, ci, :], BK.f32(b)[:, 0:2], fl, None, ALU.mult, None, [bs[0], st["C"]["slot"]], [st["hss"]])
                            ts("dve", st["hstore"][:, FC + ci, :], BK.f32(b + 1)[:, 0:2], fl, None, ALU.mult, None, [bs[1], st["C"]["slot"]], [st["hss"]])
                        res.append((ml, epi_h, [st["hhs"]]))
                    for tg in range(T // tgw):
                        ml = [(0, (lambda b, w=tgw: BK.f32(b)[:, 0:w]), wt[:, k, sub * 128:(sub + 1) * 128], at[:, k, tg * tgw:(tg + 1) * tgw]) for k in range(KC)]
                        ml += [(1, (lambda b, w=tgw: BK.f32(b)[:, 0:w]), wt[:, k, 256 + sub * 128:256 + (sub + 1) * 128], at[:, k, tg * tgw:(tg + 1) * tgw]) for k in range(KC)]

                        def epi(b, bs, ci=ci, tt0=t0 + tg * tgw, w=tgw):
                            ue, ues = st["ue"][st["i"] % 2]
                            st["i"] += 1
                            wc, bc, wcs = st["wc"], st["bc"], st["wcs"]
                            outs = []
                            for part, (cidx, dst, dsts) in enumerate(((ci, st["cg"], st["cgs"]), (FC + ci, st["cu"], st["cus"]))):
                                u = ue[:, part, :]
                                cp("dve", u[:, 0:2], st["hstore"][:, cidx, :], [st["hss"]], [ues])
                                cp("act", u[:, 2:2 + w], BK.f32(b + part)[:, 0:w], [bs[part]], [ues])
                                cp("dve", st["hstore"][:, cidx, :], u[:, w:w + 2], [ues], [st["hss"]])
                                act(dst[:, 0:w], u[:, 2:2 + w], AF.Identity, [ues, wcs], [dsts], bias=bc[:, cidx:cidx + 1], scale=wc[:, 2, cidx:cidx + 1])
                                stt("dve", dst[:, 0:w], u[:, 1:1 + w], wc[:, 1, cidx:cidx + 1], dst[:, 0:w], ALU.mult, ALU.add, [ues, wcs, dsts], [dsts])
                                stt("dve", dst[:, 0:w], u[:, 0:w], wc[:, 0, cidx:cidx + 1], dst[:, 0:w], ALU.mult, ALU.add, [ues, wcs, dsts], [dsts])
                            cg, cu, t1, sg = st["cg"], st["cu"], st["t1"], st["sg"]
                            tt("dve", t1[:, 0:w], cg[:, 0:w], cg[:, 0:w], ALU.mult, [st["cgs"]], [st["t1s"]])
                            ts("dve", t1[:, 0:w], t1[:, 0:w], 0.044715, 1.0, ALU.mult, ALU.add, [st["t1s"]], [st["t1s"]])
                            tt("dve", t1[:, 0:w], t1[:, 0:w], cg[:, 0:w], ALU.mult, [st["t1s"], st["cgs"]], [st["t1s"]])
                            act(sg[:, 0:w], t1[:, 0:w], AF.Sigmoid, [st["t1s"]], [st["sgs"]], scale=1.5957691216057308)
                            tt("dve", sg[:, 0:w], sg[:, 0:w], cg[:, 0:w], ALU.mult, [st["sgs"], st["cgs"]], [st["sgs"]])
                            so, sos = st["out"].next()
                            tt("dve", so[:, 0:w], sg[:, 0:w], cu[:, 0:w], ALU.mult, [st["sgs"], st["cus"]], [sos])
                            P.dma("sp", aT[ci, :, tt0:tt0 + w], so[:, 0:w], reads=[sos])
                        res.append((ml, epi))
                return res
            return f

        blocks = []
        for c0 in range(0, DFF, 256):
            cw = min(256, DFF - c0)
            blocks.append(dict(parts=[(w_up, 0, KC, c0, cw, 0, 0), (w_up, 0, KC, DFF + c0, cw, 0, 256)], tiles=tiles_for(c0, cw)))
        gemm_phase(hT, KC, S, T, blocks, extra_setup=setup, nwbuf=2)

    if upto > 17:
        g6()
    if upto > 18:
        tm_gemm_to_y(aT, FC, w_dn, min(512, S), kparts=2)
    if upto > 19:
        nr_phase(out, S, yb, 2, 0, out, None)

    with nc.Block() as block:
        P.emit(block)
    es.close()
    _NC_CACHE["P"] = P
    return nc


def make_consts():
    s = np.arange(128)[:, None]
    t = np.arange(128)[None, :]
    ident = (s == t).astype(np.float32)
    triu = (s <= t).astype(np.float32)
    tril = (s > t).astype(np.float32)
    ones = np.ones((128, 128), np.float32)
    return np.ascontiguousarray(np.concatenate([ident, triu * (-1.0 / 16), tril * (-1.0 / 16), triu, ones], axis=1))


def make_in_maps(cfg, inp, n_cores=8):
    D, S, KC, FC, GH = cfg["D"], cfg["S"], cfg["KC"], cfg["FC"], cfg["GH"]
    f = lambda a: np.ascontiguousarray(np.asarray(a, dtype=np.float32))
    col = lambda v: f(np.asarray(v).reshape(-1, 128).T)
    L0 = lambda k: np.asarray(inp[k])[0]
    shared = {
        "w_in": f(L0("w_in")), "w_gla_branch": f(L0("w_gla_branch")), "w_fox_branch": f(L0("w_fox_branch")),
        "w_out": f(L0("w_out")), "w_xa_q": f(L0("w_xa_q")), "w_xa_kv": f(L0("w_xa_kv")), "w_xa_o": f(L0("w_xa_o")),
        "w_ffn_up": f(L0("w_ffn_up")), "w_ffn_down": f(L0("w_ffn_down")),
        "gcols": f(np.concatenate([col(L0("g_mix_pre")), col(L0("g_xa_pre")), col(L0("g_mem")), col(L0("g_ffn_pre"))], axis=1)),
        "gpost": f(np.stack([L0("g_mix_post"), L0("g_xa_post"), L0("g_ffn_post")])),
        "bgate": col(L0("b_gate")),
        "wgu": f(np.concatenate([L0("w_gla_gate_up"), L0("b_gla_gate")[None, :]], axis=0)),
        "ggla": f(np.tile(L0("g_gla_norm"), GH)[None, :]),
        "bfox": f(L0("b_fox_f")[None, :]),
        "wconv": f(np.stack([col(L0("w_conv")[j]) for j in range(3)], axis=1).reshape(128, -1)),
        "bconv": col(L0("b_conv")),
        "consts": make_consts(),
    }
    xs = np.asarray(inp["x"])
    ms = np.asarray(inp["mem"])
    maps = []
    for c in range(n_cores):
        b, half = c // 2, c % 2
        m = dict(shared)
        m["x"] = f(xs[b, half * S:(half + 1) * S])
        m["mem"] = f(ms[b])
        fl = np.zeros((128, 2), np.float32)
        fl[:, 0] = float(half)
        fl[:, 1] = (float(half) - 1.0) * (-NEG_BIG)
        m["flags"] = fl
        maps.append(m)
    return maps


_NC_CACHE = {}


def kernel(**inputs):
    cfg = make_cfg()
    if "nc" not in _NC_CACHE:
        _NC_CACHE["nc"] = build_program(cfg)
    nc = _NC_CACHE["nc"]
    maps = make_in_maps(cfg, inputs)
    res = run_bass_kernel_spmd(nc, maps, core_ids=list(range(8)))
    B = np.asarray(inputs["x"]).shape[0]
    S = cfg["S"]
    outp = np.empty((B, 2 * S, cfg["D"]), np.float32)
    for c in range(8):
        outp[c // 2, (c % 2) * S:(c % 2 + 1) * S] = res.results[c]["out"]
    return outp
```
